# Optimizing a Trainium2 kernel written in Bass

```python
import jax, jax.numpy as jnp
from jax import lax
import numpy as np

D_MODEL = 1024
BATCH = 8
SEQ = 4096
DEPTH = 4

ROPE_THETA = 10000.0
NORM_EPS = 1e-6
NEG_INF = -1e30

A_HEADS = 8
A_GROUPS = 2
A_HPG = A_HEADS // A_GROUPS
A_HEAD_DIM = 64
CMP_LEN = 32
CMP_STRIDE = 16
SEL_LEN = 64
SEL_TOPK = 16
WINDOW = 512
Q_CHUNK = 64

B_HEADS = 8
B_Q_LORA = 384
B_KV_LORA = 256
B_NOPE = 64
B_ROPE = 32
B_V = 64
B_QK = B_NOPE + B_ROPE
ATTN_BLOCK = 128

C_WIDTH = 1024
C_BLOCKS = 8
C_BLOCK_W = C_WIDTH // C_BLOCKS
C_CONV = 4
C_SCALE = 8.0

FFN_HIDDEN = -(-8 * D_MODEL // (3 * 256)) * 256

A_Q = A_HEADS * A_HEAD_DIM
A_KV = A_GROUPS * A_HEAD_DIM
A_GATES = 3 * A_HEADS
IN_SPLITS = (A_Q, A_KV, A_KV, A_KV, A_KV, A_KV, A_KV, A_GATES,
             B_Q_LORA, B_KV_LORA, B_ROPE,
             C_WIDTH, C_WIDTH,
             D_MODEL, D_MODEL, D_MODEL)
N_IN = sum(IN_SPLITS)
A_OUT = A_HEADS * A_HEAD_DIM
B_OUT = B_HEADS * B_V

kernel_name = 'hybrid_nsa_mla_rglru_gated_block'


def rms_norm(x, g):
    xf = x.astype(jnp.float32)
    y = xf * lax.rsqrt(jnp.mean(xf * xf, axis=-1, keepdims=True) + NORM_EPS)
    return (y * g.astype(jnp.float32)).astype(x.dtype)


def rope(x, pos):
    half = x.shape[-1] // 2
    inv = ROPE_THETA ** (-jnp.arange(half, dtype=jnp.float32) / half)
    ang = pos.astype(jnp.float32)[..., None] * inv
    cos = jnp.cos(ang)[:, :, None, :]
    sin = jnp.sin(ang)[:, :, None, :]
    xf = x.astype(jnp.float32)
    x1, x2 = xf[..., :half], xf[..., half:]
    return jnp.concatenate([x1 * cos - x2 * sin, x2 * cos + x1 * sin], axis=-1).astype(x.dtype)


def masked_softmax(s, mask):
    s = jnp.where(mask, s.astype(jnp.float32), NEG_INF)
    p = jax.nn.softmax(s, axis=-1)
    return jnp.where(mask, p, 0.0)


def nsa_mixer(q, k_cmp, v_cmp, k_sel, v_sel, k_win, v_win, gate_logits, pos,
              q_norm_g, k_norm_g, cmp_pos, cmp_w1, cmp_w2):
    bsz, S, _ = q.shape
    dt = q.dtype
    G, HPG, dh = A_GROUPS, A_HPG, A_HEAD_DIM
    scale = dh ** -0.5
    t = jnp.arange(S, dtype=jnp.int32)
    q = rms_norm(q.reshape(bsz, S, A_HEADS, dh), q_norm_g)

    n_cmp = (S - CMP_LEN) // CMP_STRIDE + 1
    starts = np.arange(n_cmp) * CMP_STRIDE
    blk_idx = starts[:, None] + np.arange(CMP_LEN)[None, :]
    kv_raw = jnp.stack([k_cmp, v_cmp], 0).reshape(2, bsz, S, G, dh)
    blocks = jnp.take(kv_raw, blk_idx, axis=2)
    blocks = blocks + cmp_pos[:, None, None, :, None, :]
    blocks = blocks.transpose(0, 1, 2, 4, 3, 5).reshape(2, bsz, n_cmp, G, CMP_LEN * dh)
    hid = jax.nn.silu(jnp.einsum('zbngf,zfe->zbnge', blocks, cmp_w1))
    comp = jnp.einsum('zbnge,zef->zbngf', hid, cmp_w2)
    k_c = rms_norm(comp[0], k_norm_g)
    v_c = comp[1]
    qg = q.reshape(bsz, S, G, HPG, dh)
    s_c = jnp.einsum('bsghd,bngd->bghsn', qg, k_c) * scale
    mask_c = t[:, None] >= jnp.asarray(starts + CMP_LEN - 1, dtype=jnp.int32)[None, :]
    p_c = masked_softmax(s_c, mask_c)
    o_c = jnp.einsum('bghsn,bngd->bsghd', p_c.astype(dt), v_c)

    n_sel = S // SEL_LEN
    k_top = min(SEL_TOPK, n_sel)
    sel_starts = np.arange(n_sel) * SEL_LEN
    ovl = np.clip(np.minimum(starts[:, None] + CMP_LEN, sel_starts[None, :] + SEL_LEN)
                  - np.maximum(starts[:, None], sel_starts[None, :]), 0, None)
    ovl = jnp.asarray(ovl / CMP_LEN, dtype=jnp.float32)
    imp = jnp.einsum('bghsn,nj->bgsj', p_c, ovl)
    cur = (t // SEL_LEN)[:, None]
    j = jnp.arange(n_sel, dtype=jnp.int32)[None, :]
    forced = (j == 0) | (j == cur) | (j == cur - 1)
    imp = jnp.where(j > cur, NEG_INF, jnp.where(forced, -NEG_INF, imp))
    _, sel_idx = lax.top_k(imp, k_top)

    q_r = rope(q, pos).reshape(bsz, S, G, HPG, dh)
    k_s = rope(rms_norm(k_sel.reshape(bsz, S, G, dh), k_norm_g), pos)
    k_w = rope(rms_norm(k_win.reshape(bsz, S, G, dh), k_norm_g), pos)
    v_s = v_sel.reshape(bsz, S, G, dh)
    v_w = v_win.reshape(bsz, S, G, dh)
    k_s_blk = k_s.reshape(bsz, n_sel, SEL_LEN, G, dh).transpose(0, 3, 1, 2, 4)
    v_s_blk = v_s.reshape(bsz, n_sel, SEL_LEN, G, dh).transpose(0, 3, 1, 2, 4)
    pad = ((0, 0), (WINDOW, 0), (0, 0), (0, 0))
    k_w_pad = jnp.pad(k_w, pad)
    v_w_pad = jnp.pad(v_w, pad)
    b_ix = jnp.arange(bsz)[:, None, None, None]
    g_ix = jnp.arange(G)[None, :, None, None]
    sel_off = jnp.arange(SEL_LEN, dtype=jnp.int32)
    win_off = jnp.arange(WINDOW + Q_CHUNK, dtype=jnp.int32)
    n_keys = k_top * SEL_LEN

    def chunk(args):
        qc, idx, ci = args
        start = ci * Q_CHUNK
        tq = start + jnp.arange(Q_CHUNK, dtype=jnp.int32)
        ks = k_s_blk[b_ix, g_ix, idx].reshape(bsz, G, Q_CHUNK, n_keys, dh)
        vs = v_s_blk[b_ix, g_ix, idx].reshape(bsz, G, Q_CHUNK, n_keys, dh)
        kpos = (idx[..., None] * SEL_LEN + sel_off).reshape(bsz, G, Q_CHUNK, n_keys)
        s = jnp.einsum('bqghd,bgqkd->bghqk', qc, ks) * scale
        p = masked_softmax(s, kpos[:, :, None] <= tq[:, None])
        o_s = jnp.einsum('bghqk,bgqkd->bqghd', p.astype(dt), vs)
        kw = lax.dynamic_slice_in_dim(k_w_pad, start, WINDOW + Q_CHUNK, axis=1)
        vw = lax.dynamic_slice_in_dim(v_w_pad, start, WINDOW + Q_CHUNK, axis=1)
        wpos = start - WINDOW + win_off
        mask_w = (wpos[None, :] <= tq[:, None]) & (wpos[None, :] > tq[:, None] - WINDOW) & (wpos[None, :] >= 0)
        s_w = jnp.einsum('bqghd,bkgd->bghqk', qc, kw) * scale
        p_w = masked_softmax(s_w, mask_w)
        o_w = jnp.einsum('bghqk,bkgd->bqghd', p_w.astype(dt), vw)
        return o_s, o_w

    n_chunks = S // Q_CHUNK
    xs = (q_r.reshape(bsz, n_chunks, Q_CHUNK, G, HPG, dh).swapaxes(0, 1),
          sel_idx.reshape(bsz, G, n_chunks, Q_CHUNK, k_top).transpose(2, 0, 1, 3, 4),
          jnp.arange(n_chunks, dtype=jnp.int32))
    o_s, o_w = lax.map(chunk, xs)
    o_s = o_s.swapaxes(0, 1).reshape(bsz, S, G, HPG, dh)
    o_w = o_w.swapaxes(0, 1).reshape(bsz, S, G, HPG, dh)

    g = jax.nn.sigmoid(gate_logits).reshape(bsz, S, G, HPG, 3)
    o = g[..., 0:1] * o_c + g[..., 1:2] * o_s + g[..., 2:3] * o_w
    return o.reshape(bsz, S, A_OUT)


def mla_mixer(c_q, c_kv, k_pe, pos, cq_norm_g, ckv_norm_g, w_uq, w_ukv, q_norm_g, k_norm_g):
    bsz, S, _ = c_q.shape
    dt = c_q.dtype
    q = (rms_norm(c_q, cq_norm_g) @ w_uq).reshape(bsz, S, B_HEADS, B_QK)
    kv = (rms_norm(c_kv, ckv_norm_g) @ w_ukv).reshape(bsz, S, B_HEADS, B_NOPE + B_V)
    k_nope, v = kv[..., :B_NOPE], kv[..., B_NOPE:]
    k = jnp.concatenate([k_nope, jnp.broadcast_to(k_pe[:, :, None, :], (bsz, S, B_HEADS, B_ROPE))], axis=-1)
    q = rms_norm(q, q_norm_g)
    k = rms_norm(k, k_norm_g)
    q = jnp.concatenate([q[..., :B_NOPE], rope(q[..., B_NOPE:], pos)], axis=-1)
    k = jnp.concatenate([k[..., :B_NOPE], rope(k[..., B_NOPE:], pos)], axis=-1)
    scale = B_QK ** -0.5
    kpos = jnp.arange(S, dtype=jnp.int32)
    n_blk = S // ATTN_BLOCK

    def blk(args):
        qi, bi = args
        tq = bi * ATTN_BLOCK + jnp.arange(ATTN_BLOCK, dtype=jnp.int32)
        s = jnp.einsum('bqhd,bkhd->bhqk', qi, k) * scale
        p = masked_softmax(s, kpos[None, :] <= tq[:, None])
        return jnp.einsum('bhqk,bkhd->bqhd', p.astype(dt), v)

    qb = q.reshape(bsz, n_blk, ATTN_BLOCK, B_HEADS, B_QK).swapaxes(0, 1)
    o = lax.map(blk, (qb, jnp.arange(n_blk, dtype=jnp.int32)))
    return o.swapaxes(0, 1).reshape(bsz, S, B_OUT)


def rglru_mixer(gate, xr, conv_w, conv_b, w_a, b_a, w_x, b_x, lam):
    bsz, S, C = xr.shape
    dt = xr.dtype
    xc = lax.conv_general_dilated(xr, conv_w[:, None, :], window_strides=(1,),
                                  padding=[(C_CONV - 1, 0)],
                                  dimension_numbers=('NWC', 'WIO', 'NWC'),
                                  feature_group_count=C) + conv_b
    xb = xc.reshape(bsz, S, C_BLOCKS, C_BLOCK_W)
    r = jax.nn.sigmoid(jnp.einsum('bsnc,ncd->bsnd', xb, w_a).reshape(bsz, S, C) + b_a)
    i = jax.nn.sigmoid(jnp.einsum('bsnc,ncd->bsnd', xb, w_x).reshape(bsz, S, C) + b_x)
    log_a = -C_SCALE * r.astype(jnp.float32) * jax.nn.softplus(-lam.astype(jnp.float32))
    a = jnp.exp(log_a)
    b = jnp.sqrt(-jnp.expm1(2.0 * log_a)) * (i.astype(jnp.float32) * xc.astype(jnp.float32))

    def combine(e1, e2):
        a1, b1 = e1
        a2, b2 = e2
        return a1 * a2, a2 * b1 + b2

    _, h = lax.associative_scan(combine, (a, b), axis=1)
    return jax.nn.gelu(gate) * h.astype(dt)


def setup_inputs(seed: int = 0) -> dict:
    key = jax.random.key(seed)
    ks = jax.random.split(key, 30)
    f32 = jnp.float32
    L = DEPTH

    def nrm(k, shape, fan_in):
        return jax.random.normal(k, shape, f32) * (fan_in ** -0.5)

    def gain(k, shape):
        return 1.0 + 0.01 * jax.random.normal(k, shape, f32)

    x = jax.random.normal(ks[0], (BATCH, SEQ, D_MODEL), f32)
    positions = (jnp.arange(SEQ, dtype=jnp.int32)[None, :]
                 + jax.random.randint(ks[1], (BATCH, 1), 0, 1024, dtype=jnp.int32))
    u = jax.random.uniform(ks[21], (L, C_WIDTH), f32, 0.9, 0.999)
    s_lam = u ** (1.0 / C_SCALE)
    return {
        'x': x,
        'positions': positions,
        'mix_norm_g': gain(ks[2], (L, D_MODEL)),
        'w_in': nrm(ks[3], (L, D_MODEL, N_IN), D_MODEL),
        'a_q_norm_g': gain(ks[4], (L, A_HEAD_DIM)),
        'a_k_norm_g': gain(ks[5], (L, A_HEAD_DIM)),
        'a_cmp_pos': 0.02 * jax.random.normal(ks[6], (L, 2, CMP_LEN, A_HEAD_DIM), f32),
        'a_cmp_w1': nrm(ks[7], (L, 2, CMP_LEN * A_HEAD_DIM, A_HEAD_DIM), CMP_LEN * A_HEAD_DIM),
        'a_cmp_w2': nrm(ks[8], (L, 2, A_HEAD_DIM, A_HEAD_DIM), A_HEAD_DIM),
        'b_cq_norm_g': gain(ks[9], (L, B_Q_LORA)),
        'b_ckv_norm_g': gain(ks[10], (L, B_KV_LORA)),
        'b_w_uq': nrm(ks[11], (L, B_Q_LORA, B_HEADS * B_QK), B_Q_LORA),
        'b_w_ukv': nrm(ks[12], (L, B_KV_LORA, B_HEADS * (B_NOPE + B_V)), B_KV_LORA),
        'b_q_norm_g': gain(ks[13], (L, B_QK)),
        'b_k_norm_g': gain(ks[14], (L, B_QK)),
        'c_conv_w': nrm(ks[15], (L, C_CONV, C_WIDTH), C_CONV),
        'c_conv_b': 0.01 * jax.random.normal(ks[16], (L, C_WIDTH), f32),
        'c_w_a': nrm(ks[17], (L, C_BLOCKS, C_BLOCK_W, C_BLOCK_W), C_BLOCK_W),
        'c_b_a': 0.01 * jax.random.normal(ks[18], (L, C_WIDTH), f32),
        'c_w_x': nrm(ks[19], (L, C_BLOCKS, C_BLOCK_W, C_BLOCK_W), C_BLOCK_W),
        'c_b_x': 0.01 * jax.random.normal(ks[20], (L, C_WIDTH), f32),
        'c_lambda': jnp.log(s_lam) - jnp.log1p(-s_lam),
        'w_pa': nrm(ks[22], (L, A_OUT, D_MODEL), A_OUT),
        'w_pb': nrm(ks[23], (L, B_OUT, D_MODEL), B_OUT),
        'w_pc': nrm(ks[24], (L, C_WIDTH, D_MODEL), C_WIDTH),
        'w_o': nrm(ks[25], (L, D_MODEL, D_MODEL), D_MODEL),
        'ffn_norm_g': gain(ks[26], (L, D_MODEL)),
        'ffn_w1': nrm(ks[27], (L, D_MODEL, FFN_HIDDEN), D_MODEL),
        'ffn_w3': nrm(ks[28], (L, D_MODEL, FFN_HIDDEN), D_MODEL),
        'ffn_w2': nrm(ks[29], (L, FFN_HIDDEN, D_MODEL), FFN_HIDDEN),
    }


def reference(x, positions, mix_norm_g, w_in, a_q_norm_g, a_k_norm_g, a_cmp_pos, a_cmp_w1, a_cmp_w2,
              b_cq_norm_g, b_ckv_norm_g, b_w_uq, b_w_ukv, b_q_norm_g, b_k_norm_g,
              c_conv_w, c_conv_b, c_w_a, c_b_a, c_w_x, c_b_x, c_lambda,
              w_pa, w_pb, w_pc, w_o, ffn_norm_g, ffn_w1, ffn_w3, ffn_w2):
    offsets = np.cumsum(IN_SPLITS)[:-1].tolist()
    for l in range(DEPTH):
        h = rms_norm(x, mix_norm_g[l])
        proj = h @ w_in[l]
        (aq, akc, avc, aks, avs, akw, avw, ag,
         bcq, bckv, bkpe, cg, cx, ma, mb, mc) = jnp.split(proj, offsets, axis=-1)
        ya = nsa_mixer(aq, akc, avc, aks, avs, akw, avw, ag, positions,
                       a_q_norm_g[l], a_k_norm_g[l], a_cmp_pos[l], a_cmp_w1[l], a_cmp_w2[l])
        yb = mla_mixer(bcq, bckv, bkpe, positions, b_cq_norm_g[l], b_ckv_norm_g[l],
                       b_w_uq[l], b_w_ukv[l], b_q_norm_g[l], b_k_norm_g[l])
        yc = rglru_mixer(cg, cx, c_conv_w[l], c_conv_b[l], c_w_a[l], c_b_a[l],
                         c_w_x[l], c_b_x[l], c_lambda[l])
        merged = (jax.nn.sigmoid(ma) * (ya @ w_pa[l])
                  + jax.nn.sigmoid(mb) * (yb @ w_pb[l])
                  + jax.nn.sigmoid(mc) * (yc @ w_pc[l]))
        x = x + merged @ w_o[l]
        h = rms_norm(x, ffn_norm_g[l])
        x = x + (jax.nn.silu(h @ ffn_w1[l]) * (h @ ffn_w3[l])) @ ffn_w2[l]
    return x
```

```python
import math
from contextlib import ExitStack

import numpy as np
import ml_dtypes
import concourse.bass as bass
import concourse.mybir as mybir
from concourse.bass_utils import run_bass_kernel_spmd

F32 = mybir.dt.float32
BF16 = mybir.dt.bfloat16
I32 = mybir.dt.int32
ALU = mybir.AluOpType
AF = mybir.ActivationFunctionType
AX = mybir.AxisListType

D = 1024
NL = 4
SEQ = 4096
EPS = 1e-6
FFN = 2816
NJ = FFN // 128
N_IN = 7096
BIG = 1e30
import os as _os
MASK_ENG = _os.environ.get("MASK_ENG", "vector")

ENGS = ("sync", "scalar", "vector", "gpsimd", "tensor")
DMA_RING = {"sync": 16, "scalar": 4, "gpsimd": 8}
EPOCH = 30000
DEPOCH = 1800


class Prog:
    def __init__(self, nc):
        self.nc = nc
        self.ops = {e: [] for e in ENGS}
        self.cnt = {e: 0 for e in ENGS}
        self.seen = {e: {} for e in ENGS}
        self.bufs = {}
        self.dma_k = {q: 0 for q in DMA_RING}
        self.semkeys = set()
        self.last = {}

    def _need(self, eng, waits, tok):
        if tok is None:
            return
        key, val, _ = tok
        if self.seen[eng].get(key, 0) >= val:
            return
        if waits.get(key, 0) < val:
            waits[key] = val

    def _deps(self, eng, reads, writes, is_dma):
        waits = {}
        for b in reads:
            st = self.bufs.get(b)
            if st is None:
                continue
            self._need(eng, waits, st["w"])
        for b in writes:
            st = self.bufs.get(b)
            if st is None:
                continue
            w = st["w"]
            if w is not None and (is_dma or w[2] != eng):
                self._need(eng, waits, w)
            for r in st["r"]:
                if is_dma or r[2] != eng:
                    self._need(eng, waits, r)
        return waits

    def _commit(self, eng, waits, tok, reads, writes):
        for k, v in waits.items():
            self.seen[eng][k] = v
        self.last[tok[0]] = max(self.last.get(tok[0], 0), tok[1])
        for b in reads:
            st = self.bufs.setdefault(b, {"w": None, "r": []})
            st["r"].append(tok)
            if len(st["r"]) > 12:
                best = {}
                for t in st["r"]:
                    if t[0] not in best or best[t[0]][1] < t[1]:
                        best[t[0]] = t
                st["r"] = list(best.values())
        for b in writes:
            self.bufs[b] = {"w": tok, "r": []}

    def op(self, eng, fn, reads=(), writes=()):
        waits = self._deps(eng, reads, writes, False)
        self.cnt[eng] += 1
        c = self.cnt[eng] - 1
        semkey = (eng, c // EPOCH)
        self.semkeys.add(semkey)
        tok = (semkey, c % EPOCH + 1, eng)
        self._commit(eng, waits, tok, reads, writes)
        self.ops[eng].append((list(waits.items()), fn, semkey, 1))
        return tok

    def dma(self, q, out, in_, reads=(), writes=(), **kw):
        waits = self._deps(q, reads, writes, True)
        k = self.dma_k[q]
        self.dma_k[q] += 1
        n = DMA_RING[q]
        use = k // n
        semkey = ("dma", q, k % n, use // DEPOCH)
        self.semkeys.add(semkey)
        val = 16 * (use % DEPOCH + 1)
        if k >= n:
            pu = use - 1
            self._need(q, waits, (("dma", q, k % n, pu // DEPOCH), 16 * (pu % DEPOCH + 1), None))
        tok = (semkey, val, None)
        self._commit(q, waits, tok, reads, writes)

        def fn(e, out=out, in_=in_, kw=kw):
            return e.dma_start(out=out, in_=in_, **kw)

        self.ops[q].append((list(waits.items()), fn, semkey, 16))
        return tok

    def barrier(self):
        toks = [(k, v, None) for k, v in self.last.items()]
        for e in ENGS:
            waits = {}
            for t in toks:
                self._need(e, waits, t)
            for k, v in waits.items():
                self.seen[e][k] = v
            self.ops[e].append((list(waits.items()), None, None, 0))
        self.bufs = {}

    def emit(self):
        nc = self.nc
        with ExitStack() as es:
            sems = {}
            for i, k in enumerate(sorted(self.semkeys, key=str)):
                sems[k] = es.enter_context(nc.semaphore(f"s{i}"))
            block = es.enter_context(nc.Block())

            def run(e_name):
                def body(e):
                    for waits, fn, semkey, inc in self.ops[e_name]:
                        for k, v in waits:
                            e.wait_ge(sems[k], v)
                        if fn is not None:
                            fn(e).then_inc(sems[semkey], inc)
                return body

            block.sync(run("sync"))
            block.scalar(run("scalar"))
            block.vector(run("vector"))
            block.gpsimd(run("gpsimd"))
            block.tensor(run("tensor"))

    def mm(self, out, lhsT, rhs, start, stop, reads, writes):
        return self.op("tensor", lambda e: e.matmul(out, lhsT=lhsT, rhs=rhs, start=start, stop=stop),
                       reads=reads, writes=writes)

    def act(self, out, in_, func, reads, writes, bias=None, scale=None):
        kw = {}
        if bias is not None:
            kw["bias"] = bias
        if scale is not None:
            kw["scale"] = scale
        return self.op("scalar", lambda e: e.activation(out=out, in_=in_, func=func, **kw),
                       reads=reads, writes=writes)

    def tt(self, eng, out, in0, in1, op, reads, writes):
        return self.op(eng, lambda e: e.tensor_tensor(out=out, in0=in0, in1=in1, op=op), reads=reads, writes=writes)

    def ts(self, eng, out, in0, s1, s2, op0, op1, reads, writes):
        if op1 is None:
            return self.op(eng, lambda e: e.tensor_scalar(out=out, in0=in0, scalar1=s1, scalar2=None, op0=op0),
                           reads=reads, writes=writes)
        return self.op(eng, lambda e: e.tensor_scalar(out=out, in0=in0, scalar1=s1, scalar2=s2, op0=op0, op1=op1),
                       reads=reads, writes=writes)

    def stt(self, eng, out, in0, scalar, in1, op0, op1, reads, writes):
        eng = "vector"
        return self.op(eng, lambda e: e.scalar_tensor_tensor(out=out, in0=in0, scalar=scalar, in1=in1, op0=op0, op1=op1),
                       reads=reads, writes=writes)

    def copy(self, eng, out, in_, reads, writes):
        if eng == "scalar":
            return self.op(eng, lambda e: e.copy(out=out, in_=in_), reads=reads, writes=writes)
        return self.op(eng, lambda e: e.tensor_copy(out=out, in_=in_), reads=reads, writes=writes)

    def recip(self, out, in_, reads, writes):
        return self.op("vector", lambda e: e.reciprocal(out=out, in_=in_), reads=reads, writes=writes)

    def memset(self, eng, ap, val, writes):
        return self.op(eng, lambda e: e.memset(ap, val), writes=writes)


def _in_groups():
    g = []
    for i in range(4):
        g.append(("q%d" % i, [(i * 128, 128, 0)]))
    g.append(("akc", [(512, 128, 0)]))
    g.append(("avc", [(640, 128, 0)]))
    g.append(("aks", [(768, 128, 0)]))
    g.append(("akw", [(1024, 128, 0)]))
    g.append(("ag", [(1280, 24, 0)]))
    for i in range(3):
        g.append(("bcq%d" % i, [(1304 + i * 128, 128, 0)]))
    for i in range(2):
        g.append(("bckv%d" % i, [(1688 + i * 128, 128, 0)]))
    g.append(("bkpe", [(1944, 32, 64)]))
    for nm, base in (("cg", 1976), ("cx", 3000), ("ma", 4024), ("mb", 5048), ("mc", 6072)):
        for i in range(8):
            g.append(("%s%d" % (nm, i), [(base + i * 128, 128, 0)]))
    return g


IN_GROUPS = _in_groups()
GIDX = {nm: i for i, (nm, _) in enumerate(IN_GROUPS)}
NG1 = len(IN_GROUPS)


def _img_km(w, kc):
    K, M = w.shape
    assert K == kc * 128
    return np.ascontiguousarray(w.reshape(kc, 128, M).transpose(1, 0, 2).reshape(128, kc * M))


def _cols(v):
    n = v.shape[0] // 128
    return np.ascontiguousarray(v.reshape(n, 128).T)


def _col_layout():
    lay = {}
    o = 0
    for nm, n in (("mix_g", 8), ("ffn_g", 8), ("aq_g", 1), ("ak_g", 1), ("bcq_g", 3), ("bckv_g", 2),
                  ("bq_g", 1), ("bk_g", 1), ("conv_w", 32), ("conv_b", 8), ("b_a", 8), ("b_x", 8),
                  ("lam", 8), ("cmp_b", 2)):
        lay[nm] = (o, n)
        o += n
    return lay, o


COL_LAY, NCOL = _col_layout()

CF_INVN, CF_INVM, CF_PI, CF_IDENT, CF_TOPW, CF_ONES, NCF = 0, 1, 2, 3, 3 + 128, 3 + 256, 3 + 256 + 64
CB_ONES, CB_BLK64, CB_RM64, CB_RM96, CB_CM, CB_CMN, CB_OVL, CB_SG, CB_ID = (
    0, 128, 256, 384, 512, 512 + 2048, 512 + 4096, 512 + 4096 + 128, 512 + 4096 + 128 + 12 * 128)
NCB = CB_ID + 128


def make_constants(T):
    cf = np.zeros((128, NCF), np.float32)
    p = np.arange(128)
    cf[:, CF_INVN] = 10000.0 ** (-(p % 32) / 32.0)
    cf[:, CF_INVM] = 10000.0 ** (-(p % 16) / 16.0)
    cf[:, CF_PI] = math.pi
    cf[:, CF_IDENT:CF_IDENT + 128] = np.eye(128)
    cf[:, CF_ONES:CF_ONES + 64] = 1.0
    tt = p[:, None]
    u = np.arange(128)[None, :]
    rel = u - 64 - (tt >= 64)
    cf[:, CF_TOPW:CF_TOPW + 128] = np.where(rel > 0, -BIG, np.where(rel >= -1, BIG, 0.0))
    cb = np.zeros((128, NCB), np.float32)
    cb[:, CB_ONES:CB_ONES + 128] = 1.0
    cb[:64, CB_BLK64:CB_BLK64 + 64] = 1.0
    cb[64:, CB_BLK64 + 64:CB_BLK64 + 128] = 1.0
    rm = np.zeros((128, 128), np.float32)
    for blk in (0, 64):
        for m in range(32):
            rm[blk + m + 32, blk + m] = -1.0
            rm[blk + m, blk + m + 32] = 1.0
    cb[:, CB_RM64:CB_RM64 + 128] = rm
    rm = np.zeros((128, 128), np.float32)
    for m in range(16):
        rm[64 + m + 16, 64 + m] = -1.0
        rm[64 + m, 64 + m + 16] = 1.0
    cb[:, CB_RM96:CB_RM96 + 128] = rm
    kk = p[:, None]
    tq = np.arange(512)[None, :]
    for d in range(4):
        cm = (d * 128 + kk <= tq).astype(np.float32)
        cb[:, CB_CM + d * 512:CB_CM + (d + 1) * 512] = cm
        cb[:, CB_CMN + d * 512:CB_CMN + (d + 1) * 512] = 1.0 - cm
    for br in range(3):
        for i in range(4):
            o = CB_SG + (br * 4 + i) * 128
            for m in range(128):
                cb[(2 * i + m // 64) * 3 + br, o + m] = 1.0
    cb[:, CB_ID:CB_ID + 128] = np.eye(128)
    n_cmp = (T - 32) // 16 + 1
    n_sel = T // 64
    starts = np.arange(n_cmp) * 16
    sel_starts = np.arange(n_sel) * 64
    ovl = np.clip(np.minimum(starts[:, None] + 32, sel_starts[None, :] + 64)
                  - np.maximum(starts[:, None], sel_starts[None, :]), 0, None) / 32.0
    ncc = (n_cmp + 127) // 128
    ovl_img = np.zeros((128, ncc, n_sel), np.float32)
    maskc = np.zeros((ncc, 128, T), np.float32)
    for c in range(ncc):
        n = np.arange(c * 128, min(n_cmp, (c + 1) * 128))
        ovl_img[:len(n), c, :] = ovl[n]
        maskc[c, :len(n), :] = (np.arange(T)[None, :] >= (16 * n + 31)[:, None])
    E = (np.arange(T)[None, :] // 64 == np.arange(64)[:, None]).astype(np.float32)
    return dict(cst_f=cf, cst_b=cb.astype(ml_dtypes.bfloat16),
                ovl=ovl_img.reshape(128, ncc * n_sel).astype(ml_dtypes.bfloat16),
                maskc=maskc.astype(ml_dtypes.bfloat16), emat=E.astype(ml_dtypes.bfloat16))


class Ctx:
    pass


def build_program(T=SEQ, nl=NL, dbg=None, stop_after=None):
    dbg = dbg or []
    nc = bass.Bass("TRN2", target_bir_lowering=False)
    P = Prog(nc)
    C = Ctx()
    C.nc, C.P, C.T, C.nl = nc, P, T, nl
    NT = T // 512
    C.NT = NT
    n_cmp = (T - 32) // 16 + 1
    n_sel = T // 64
    ncc = (n_cmp + 127) // 128
    C.n_cmp, C.n_sel, C.ncc = n_cmp, n_sel, ncc

    def din(name, shape, dt):
        return nc.dram_tensor(name, list(shape), dt, kind="ExternalInput").ap()

    def dscr(name, shape, dt):
        kind = "ExternalOutput" if name in dbg else "Internal"
        return nc.dram_tensor(name, list(shape), dt, kind=kind).ap()

    x_in = din("x", [T, D], F32)
    pos_in = din("pos", [1, T], I32)
    cst_f = din("cst_f", [128, NCF], F32)
    cst_b = din("cst_b", [128, NCB], BF16)
    ovl_in = din("ovl", [128, ncc * n_sel], BF16)
    maskc_in = din("maskc", [ncc, 128, T], BF16)
    emat_in = din("emat", [64, T], BF16)
    cols_in = din("cols", [128, nl * NCOL], F32)
    C.ovl_in, C.maskc_in, C.emat_in = ovl_in, maskc_in, emat_in
    WSPEC = {
        "w_in": (NG1, 8 * 128), "w_v": (1, 8 * 256), "w_uq": (1, 3 * 768), "w_ukvk": (1, 2 * 512),
        "w_ukvv": (1, 2 * 512), "w_ca": (1, 8 * 128), "w_cx": (1, 8 * 128), "w_pa": (8, 4 * 128),
        "w_pb": (8, 4 * 128), "w_pc": (8, 8 * 128), "w_o": (8, 8 * 128), "w_f1": (NJ, 8 * 128),
        "w_f3": (NJ, 8 * 128), "w_f2": (8, NJ * 128), "w_c1": (2, 32 * 128), "w_c2": (1, 2 * 128),
    }
    C.WSPEC = WSPEC
    wf32, wbf = {}, {}
    for nm, (ng, fl) in WSPEC.items():
        wf32[nm] = din(nm, [nl, ng, 128, fl], F32)
        wbf[nm] = dscr(nm + "_bf", [nl, ng, 128, fl], BF16)
    C.wbf = wbf
    cpos_in = din("cmp_pos", [nl, 2, 128, 32], F32)
    C.cpos_in = cpos_in
    out_d = nc.dram_tensor("out", [T, D], F32, kind="ExternalOutput").ap()

    S = {}
    S["xT"] = dscr("xT", [8, 128, T], F32)
    S["cosN"] = dscr("cosN", [128, T], F32)
    S["sinN"] = dscr("sinN", [128, T], F32)
    S["cosM"] = dscr("cosM", [32, T], F32)
    S["sinM"] = dscr("sinM", [32, T], F32)
    S["qc"] = dscr("qc", [8, 64, T], BF16)
    S["qr"] = dscr("qr", [8, 64, T], BF16)
    S["kcraw"] = dscr("kcraw", [128, T], BF16)
    S["vcraw"] = dscr("vcraw", [128, T], BF16)
    S["ks"] = dscr("ks", [2, 64, T], BF16)
    S["kw"] = dscr("kw", [2, 64, T], BF16)
    S["vsw"] = dscr("vsw", [T, 256], BF16)
    S["gT"] = dscr("gT", [24, T], F32)
    S["cqn"] = dscr("cqn", [3, 128, T], BF16)
    S["ckvn"] = dscr("ckvn", [2, 128, T], BF16)
    S["kpe"] = dscr("kpe", [32, T], F32)
    S["gc"] = dscr("gc", [8, 128, T], BF16)
    S["cx"] = dscr("cx", [8, 128, T], BF16)
    S["sga"] = dscr("sga", [8, 128, T], BF16)
    S["sgb"] = dscr("sgb", [8, 128, T], BF16)
    S["sgc"] = dscr("sgc", [8, 128, T], BF16)
    S["ya"] = dscr("ya", [8, 64, T], BF16)
    S["yb"] = dscr("yb", [8, 64, T], BF16)
    S["yc"] = dscr("yc", [8, 128, T], BF16)
    S["oc"] = dscr("oc", [8, 64, T], BF16)
    S["nmask"] = dscr("nmask", [2, 64, T], BF16)
    S["mq"] = dscr("mq", [8, 96, T], BF16)
    S["mk"] = dscr("mk", [8, 96, T], BF16)
    C.S = S

    with ExitStack() as es0:
        uniq = [0]

        def sb(name, shape, dt, es=es0):
            uniq[0] += 1
            return es.enter_context(nc.sbuf_tensor("%s_u%d" % (name, uniq[0]), list(shape), dt))

        C.sb = sb
        C.ps = [es0.enter_context(nc.psum_tensor("ps%d" % i, [128, 512], F32)) for i in range(8)]
        C.cf = sb("cf", [128, NCF], F32)
        C.cb = sb("cb", [128, NCB], BF16)
        C.cols = sb("colsT", [128, nl * NCOL], F32)
        P.dma("sync", C.cf[:], cst_f, writes=["cf"])
        P.dma("sync", C.cb[:], cst_b, writes=["cb"])
        P.dma("sync", C.cols[:], cols_in, writes=["cols"])

        def cast_layer(l):
            for nm, (ng, fl) in WSPEC.items():
                step = max(1, (1 << 20) // (128 * fl))
                for g0 in range(0, ng, step):
                    g1 = min(ng, g0 + step)
                    P.dma("gpsimd", wbf[nm][l, g0:g1], wf32[nm][l, g0:g1], writes=[("wbf", nm, l)])
        C.cast_layer = cast_layer
        cast_layer(0)

        phase0(C, x_in, pos_in)
        if stop_after == "p0":
            nl_run = 0
        else:
            nl_run = nl
        for l in range(nl_run):
            if l + 1 < nl:
                cast_layer(l + 1)
            phase1(C, l)
            if stop_after == "p1":
                break
            phase2_rglru(C, l)
            if stop_after == "p2d":
                break
            phase2_mla(C, l)
            if stop_after == "p2c":
                break
            phase2_nsa(C, l)
            if stop_after == "p2a":
                break
            phase3(C, l)
        phase_out(C, out_d)
        P.barrier()
        P.emit()
    return nc


def col(C, l, name, j=0):
    o, n = COL_LAY[name]
    b = l * NCOL + o + j
    return C.cols[:, b:b + 1]


def phase0(C, x_in, pos_in):
    P, nc, T, S = C.P, C.nc, C.T, C.S
    with ExitStack() as es:
        def sb(n, s, d):
            return C.sb(n, s, d, es)
        xt = [sb("p0_x%d" % i, [128, 1024], F32) for i in range(2)]
        stg = [sb("p0_s%d" % i, [128, 8, 512], F32) for i in range(2)]
        ident = C.cf[:, CF_IDENT:CF_IDENT + 128]
        for c in range(T // 512):
            st = stg[c % 2]
            stn = "p0_s%d" % (c % 2)
            for j in range(4):
                ti = c * 4 + j
                xb, xbn = xt[ti % 2], "p0_x%d" % (ti % 2)
                P.dma("sync", xb[:], x_in[ti * 128:(ti + 1) * 128, :], writes=[xbn])
                for half in range(2):
                    bk = half + 2 * (ti % 2)
                    for q in range(4):
                        kc = half * 4 + q
                        P.op("tensor", lambda e, o=C.ps[bk][:, q * 128:(q + 1) * 128], i=xb[:, kc * 128:(kc + 1) * 128]:
                             e.transpose(out=o, in_=i, identity=ident), reads=[xbn, "cf"], writes=["ps%d" % bk])
                    P.copy("vector" if half == 0 else "scalar",
                           st[:, half * 4:(half + 1) * 4, j * 128:(j + 1) * 128],
                           C.ps[bk][:].rearrange("p (q t) -> p q t", q=4), reads=["ps%d" % bk], writes=[stn])
            P.dma("sync", S["xT"][:, :, c * 512:(c + 1) * 512].rearrange("k p t -> p k t"), st[:],
                  reads=[stn], writes=["xT"])
        posi = sb("p0_posi", [128, T], I32)
        posf = sb("p0_posf", [128, T], F32)
        ang = sb("p0_ang", [128, T], F32)
        r = sb("p0_r", [128, T], F32)
        tb = sb("p0_tb", [128, T], F32)
        P.dma("sync", posi[:], pos_in.partition_broadcast(128), writes=["posi"])
        P.copy("vector", posf[:], posi[:], reads=["posi"], writes=["posf"])
        pi_col = C.cf[:, CF_PI:CF_PI + 1]
        for nm, invc, rows in (("N", CF_INVN, 128), ("M", CF_INVM, 32)):
            P.ts("vector", ang[:], posf[:], C.cf[:, invc:invc + 1], None, ALU.mult, None, reads=["posf", "cf"], writes=["ang"])
            for fn, shift in (("sin", 0.0), ("cos", math.pi / 2)):
                P.ts("vector", r[:], ang[:], 1.0 / (2 * math.pi), shift / (2 * math.pi), ALU.mult, ALU.add, reads=["ang"], writes=["r"])
                P.copy("vector", posi[:], r[:], reads=["r"], writes=["posi"])
                P.copy("vector", tb[:], posi[:], reads=["posi"], writes=["tb"])
                P.stt("vector", r[:], tb[:], -2 * math.pi, ang[:], ALU.mult, ALU.add, reads=["tb", "ang"], writes=["r"])
                P.ts("vector", tb[:], r[:], math.pi - shift, None, ALU.is_gt, None, reads=["r"], writes=["tb"])
                P.stt("vector", r[:], tb[:], -2 * math.pi, r[:], ALU.mult, ALU.add, reads=["tb", "r"], writes=["r"])
                P.ts("vector", r[:], r[:], shift, None, ALU.add, None, reads=["r"], writes=["r"])
                P.ts("vector", r[:], r[:], math.pi, -math.pi, ALU.min, ALU.max, reads=["r"], writes=["r"])
                P.act(tb[:], r[:], AF.Sin, reads=["r"], writes=["tb"])
                P.dma("sync", S[fn + nm], tb[0:rows, :], reads=["tb"], writes=[fn + nm])
    P.barrier()


GELU_C1 = 2.0 * 0.7978845608028654
GELU_C2 = 2.0 * 0.7978845608028654 * 0.044715


def phase1(C, l):
    P, nc, T, S, NT = C.P, C.nc, C.T, C.S, C.NT
    cb = C.cb
    ones = cb[:, CB_ONES:CB_ONES + 128]
    blk64 = cb[:, CB_BLK64:CB_BLK64 + 128]
    rm64 = cb[:, CB_RM64:CB_RM64 + 128]
    with ExitStack() as es:
        def sb(n, s, d):
            return C.sb(n, s, d, es)
        hT = sb("hT", [128, 8, T], BF16)
        cosT = sb("p1_cos", [128, T], F32)
        sinT = sb("p1_sin", [128, T], F32)
        P.dma("sync", cosT[:], S["cosN"], reads=["cosN"], writes=["p1_cos"])
        P.dma("sync", sinT[:], S["sinN"], reads=["sinN"], writes=["p1_sin"])
        xc = [sb("p1_xc%d" % i, [128, 8, 512], F32) for i in range(2)]
        sq8 = sb("p1_sq8", [128, 8, 512], BF16)
        sd = [sb("p1_sd%d" % i, [128, 512], F32) for i in range(2)]
        rstd = [sb("p1_rstd%d" % i, [128, 512], F32) for i in range(2)]
        for c in range(NT):
            sl = slice(c * 512, (c + 1) * 512)
            x_, xn = xc[c % 2], "p1_xc%d" % (c % 2)
            P.dma("sync", x_[:], S["xT"][:, :, sl].rearrange("k p t -> p k t"), reads=["xT"], writes=[xn])
            P.act(sq8[:], x_[:], AF.Square, reads=[xn], writes=["sq8"])
            for kc in range(8):
                P.mm(C.ps[7][:], ones, sq8[:, kc, :], kc == 0, kc == 7, reads=["cb", "sq8"], writes=["ps7"])
            P.act(sd[c % 2][:], C.ps[7][:], AF.Sqrt, reads=["ps7"], writes=["sd%d" % (c % 2)], bias=EPS, scale=1.0 / D)
            P.recip(rstd[c % 2][:], sd[c % 2][:], reads=["sd%d" % (c % 2)], writes=["rstd%d" % (c % 2)])
            for kc in range(8):
                P.stt("vector" if kc % 2 == 0 else "gpsimd", hT[:, kc, sl], x_[:, kc, :], col(C, l, "mix_g", kc), rstd[c % 2][:],
                      ALU.mult, ALU.mult, reads=[xn, "rstd%d" % (c % 2), "cols"], writes=["hT%d" % c])
        wts = [sb("p1_w%d" % i, [128, 8 * 128], BF16) for i in range(6)]
        wv = sb("p1_wv", [128, 8 * 256], BF16)
        tmpb = [sb("p1_tb%d" % i, [128, 512], BF16) for i in range(4)]
        tmpf = [sb("p1_tf%d" % i, [128, 512], F32) for i in range(6)]
        outb = [sb("p1_ob%d" % i, [128, 512], BF16) for i in range(4)]
        outf = [sb("p1_of%d" % i, [128, 512], F32) for i in range(2)]
        vout = [sb("p1_vo%d" % i, [128, 256], BF16) for i in range(2)]
        state = {"w": 0, "ob": 0, "of": 0, "tb": 0, "tf": 0}

        def nxt(kind, n):
            i = state[kind]
            state[kind] = (i + 1) % n
            return i

        def load_w(gi):
            i = nxt("w", 6)
            P.dma("sync", wts[i][:], C.wbf["w_in"][l, gi], reads=[("wbf", "w_in", l)], writes=["p1_w%d" % i])
            return i

        def proj(wi, c, bank, M=128):
            sl = slice(c * 512, (c + 1) * 512)
            for kc in range(8):
                P.mm(C.ps[bank][0:M, :], wts[wi][:, kc * 128:kc * 128 + M], hT[:, kc, sl], kc == 0, kc == 7,
                     reads=["p1_w%d" % wi, "hT%d" % c], writes=["ps%d" % bank])

        def get_ob():
            i = nxt("ob", 4)
            return outb[i], "p1_ob%d" % i

        def get_tf():
            i = nxt("tf", 6)
            return tmpf[i], "p1_tf%d" % i

        def get_tb():
            i = nxt("tb", 4)
            return tmpb[i], "p1_tb%d" % i

        def rstd_from(ps_ss, n, rows=128):
            t, tn = get_tf()
            P.act(t[0:rows, :], ps_ss, AF.Sqrt, reads=["ps_any"], writes=[tn], bias=EPS, scale=1.0 / n)
            r_, rn = get_tf()
            P.recip(r_[0:rows, :], t[0:rows, :], reads=[tn], writes=[rn])
            return r_, rn

        C.p1_helpers = None
        def simple(gname, dst, func, M=128, r0=0, f32out=False):
            gi = GIDX[gname]
            wi = load_w(gi)
            for c in range(NT):
                bank = c % 2
                sl = slice(c * 512, (c + 1) * 512)
                proj(wi, c, bank, M)
                if f32out:
                    i = nxt("of", 2)
                    o, on = outf[i], "p1_of%d" % i
                else:
                    o, on = get_ob()
                P.act(o[r0:M, :], C.ps[bank][r0:M, :], func, reads=["ps%d" % bank], writes=[on])
                P.dma("sync", dst[:, sl], o[r0:M, :], reads=[on], writes=[("scr", gname)])

        def rms_rope(gname, gcol, dst_rope, dst_plain):
            gi = GIDX[gname]
            wi = load_w(gi)
            for c in range(NT):
                bank = c % 2
                sl = slice(c * 512, (c + 1) * 512)
                proj(wi, c, bank)
                sq, sqn = get_tb()
                P.act(sq[:], C.ps[bank][:], AF.Square, reads=["ps%d" % bank], writes=[sqn])
                P.mm(C.ps[2 + bank][:], blk64, sq[:], True, True, reads=["cb", sqn], writes=["ps%d" % (2 + bank)])
                t, tn = get_tf()
                P.act(t[:], C.ps[2 + bank][:], AF.Sqrt, reads=["ps%d" % (2 + bank)], writes=[tn], bias=EPS, scale=1.0 / 64)
                r_, rn = get_tf()
                P.recip(r_[:], t[:], reads=[tn], writes=[rn])
                qn, qnn = get_tf()
                P.stt("vector", qn[:], C.ps[bank][:], gcol, r_[:], ALU.mult, ALU.mult, reads=["ps%d" % bank, rn, "cols"], writes=[qnn])
                qb, qbn = get_ob()
                P.copy("scalar", qb[:], qn[:], reads=[qnn], writes=[qbn])
                if dst_plain is not None:
                    P.dma("sync", dst_plain[:, sl], qb[:], reads=[qbn], writes=[("scr", gname, "plain")])
                P.mm(C.ps[4 + bank][:], rm64, qb[:], True, True, reads=["cb", qbn], writes=["ps%d" % (4 + bank)])
                t1, t1n = get_tf()
                P.tt("gpsimd", t1[:], qn[:], cosT[:, sl], ALU.mult, reads=[qnn, "p1_cos"], writes=[t1n])
                t2, t2n = get_tf()
                P.tt("vector", t2[:], C.ps[4 + bank][:], sinT[:, sl], ALU.mult, reads=["ps%d" % (4 + bank), "p1_sin"], writes=[t2n])
                o, on = get_ob()
                P.tt("gpsimd", o[:], t1[:], t2[:], ALU.add, reads=[t1n, t2n], writes=[on])
                P.dma("sync", dst_rope[:, sl], o[:], reads=[on], writes=[("scr", gname, "rope")])

        def rms_multi(gnames, gcolname, dst, n):
            wis = [load_w(GIDX[g]) for g in gnames]
            ng = len(gnames)
            for c in range(NT):
                sl = slice(c * 512, (c + 1) * 512)
                for j in range(ng):
                    proj(wis[j], c, j)
                for j in range(ng):
                    sq, sqn = get_tb()
                    P.act(sq[:], C.ps[j][:], AF.Square, reads=["ps%d" % j], writes=[sqn])
                    P.mm(C.ps[3][:], ones, sq[:], j == 0, j == ng - 1, reads=["cb", sqn], writes=["ps3"])
                t, tn = get_tf()
                P.act(t[:], C.ps[3][:], AF.Sqrt, reads=["ps3"], writes=[tn], bias=EPS, scale=1.0 / n)
                r_, rn = get_tf()
                P.recip(r_[:], t[:], reads=[tn], writes=[rn])
                for j in range(ng):
                    o, on = get_ob()
                    P.stt("vector", o[:], C.ps[j][:], col(C, l, gcolname, j), r_[:], ALU.mult, ALU.mult,
                          reads=["ps%d" % j, rn, "cols"], writes=[on])
                    P.dma("sync", dst[j, :, sl], o[:], reads=[on], writes=[("scr", gnames[j])])

        def gelu_group(gname, dst):
            gi = GIDX[gname]
            wi = load_w(gi)
            for c in range(NT):
                bank = c % 2
                sl = slice(c * 512, (c + 1) * 512)
                proj(wi, c, bank)
                x2, x2n = get_tf()
                P.act(x2[:], C.ps[bank][:], AF.Square, reads=["ps%d" % bank], writes=[x2n])
                P.ts("gpsimd", x2[:], x2[:], GELU_C2, GELU_C1, ALU.mult, ALU.add, reads=[x2n], writes=[x2n])
                inner, inn = get_tf()
                P.tt("vector", inner[:], x2[:], C.ps[bank][:], ALU.mult, reads=[x2n, "ps%d" % bank], writes=[inn])
                P.act(inner[:], inner[:], AF.Sigmoid, reads=[inn], writes=[inn])
                o, on = get_ob()
                P.tt("vector", o[:], inner[:], C.ps[bank][:], ALU.mult, reads=[inn, "ps%d" % bank], writes=[on])
                P.dma("sync", dst[:, sl], o[:], reads=[on], writes=[("scr", gname)])

        qc2 = S["qc"].rearrange("h d t -> (h d) t")
        qr2 = S["qr"].rearrange("h d t -> (h d) t")
        for i in range(4):
            rms_rope("q%d" % i, col(C, l, "aq_g"), qr2[i * 128:(i + 1) * 128, :], qc2[i * 128:(i + 1) * 128, :])
        rms_rope("aks", col(C, l, "ak_g"), S["ks"].rearrange("g d t -> (g d) t"), None)
        rms_rope("akw", col(C, l, "ak_g"), S["kw"].rearrange("g d t -> (g d) t"), None)
        simple("akc", S["kcraw"], AF.Copy)
        simple("avc", S["vcraw"], AF.Copy)
        simple("ag", S["gT"], AF.Sigmoid, M=24, f32out=True)
        simple("bkpe", S["kpe"], AF.Copy, M=96, r0=64, f32out=True)
        rms_multi(["bcq0", "bcq1", "bcq2"], "bcq_g", S["cqn"], 384)
        rms_multi(["bckv0", "bckv1"], "bckv_g", S["ckvn"], 256)
        for i in range(8):
            gelu_group("cg%d" % i, S["gc"][i])
        for i in range(8):
            simple("cx%d" % i, S["cx"][i], AF.Copy)
        for nm, dst in (("ma", "sga"), ("mb", "sgb"), ("mc", "sgc")):
            for i in range(8):
                simple("%s%d" % (nm, i), S[dst][i], AF.Sigmoid)
        P.dma("sync", wv[:], C.wbf["w_v"][l, 0], reads=[("wbf", "w_v", l)], writes=["p1_wv"])
        for i in range(T // 128):
            bank = 6 + (i % 2)
            for kc in range(8):
                P.mm(C.ps[bank][:, 0:256], hT[:, kc, i * 128:(i + 1) * 128], wv[:, kc * 256:(kc + 1) * 256], kc == 0, kc == 7,
                     reads=["hT%d" % (i // 4), "p1_wv"], writes=["ps%d" % bank])
            P.copy("scalar", vout[i % 2][:], C.ps[bank][:, 0:256], reads=["ps%d" % bank], writes=["p1_vo%d" % (i % 2)])
            P.dma("sync", S["vsw"][i * 128:(i + 1) * 128, :], vout[i % 2][:], reads=["p1_vo%d" % (i % 2)], writes=["vsw"])
    P.barrier()


def phase3(C, l):
    P, nc, T, S, NT = C.P, C.nc, C.T, C.S, C.NT
    ones = C.cb[:, CB_ONES:CB_ONES + 128]
    with ExitStack() as es:
        def sb(n, s, d):
            return C.sb(n, s, d, es)
        ya = sb("p3_ya", [128, 4, 512], BF16)
        yb = sb("p3_yb", [128, 4, 512], BF16)
        yc = sb("p3_yc", [128, 8, 512], BF16)
        sg = [sb("p3_sg%d" % i, [128, 8, 512], BF16) for i in range(3)]
        xc = sb("p3_xc", [128, 8, 512], F32)
        mg = sb("p3_mg", [128, 8, 512], BF16)
        x1 = sb("p3_x1", [128, 8, 512], F32)
        sq8 = sb("p3_sq8", [128, 8, 512], BF16)
        h2 = sb("p3_h2", [128, 8, 512], BF16)
        actb = sb("p3_act", [128, NJ, 512], BF16)
        wsl = [sb("p3_w%d" % i, [128, 8 * 128], BF16) for i in range(6)]
        w2sl = [sb("p3_w2%d" % i, [128, NJ * 128], BF16) for i in range(2)]
        tf = [sb("p3_tf%d" % i, [128, 512], F32) for i in range(6)]
        st = {"w": 0, "w2": 0, "tf": 0}

        def nxt(k, n):
            i = st[k]
            st[k] = (i + 1) % n
            return i

        def loadw(name, g, kcn):
            i = nxt("w", 6)
            P.dma("sync", wsl[i][:, 0:kcn * 128], C.wbf[name][l, g], reads=[("wbf", name, l)], writes=["p3_w%d" % i])
            return wsl[i], "p3_w%d" % i

        def gtf():
            i = nxt("tf", 6)
            return tf[i], "p3_tf%d" % i

        for c in range(NT):
            sl = slice(c * 512, (c + 1) * 512)
            P.dma("sync", ya[:], S["ya"].rearrange("h d t -> (h d) t")[:, sl].rearrange("(k p) t -> p k t", p=128), reads=["ya"], writes=["p3_ya"])
            P.dma("sync", yb[:], S["yb"].rearrange("h d t -> (h d) t")[:, sl].rearrange("(k p) t -> p k t", p=128), reads=["yb"], writes=["p3_yb"])
            P.dma("sync", yc[:], S["yc"][:, :, sl].rearrange("k p t -> p k t"), reads=["yc"], writes=["p3_yc"])
            for i, nm in enumerate(("sga", "sgb", "sgc")):
                P.dma("sync", sg[i][:], S[nm][:, :, sl].rearrange("k p t -> p k t"), reads=[nm], writes=["p3_sg%d" % i])
            P.dma("sync", xc[:], S["xT"][:, :, sl].rearrange("k p t -> p k t"), reads=["xT"], writes=["p3_xc"])
            for m in range(8):
                b0 = (m % 2) * 3
                for bi, (wn, src, srcn, kcn) in enumerate((("w_pa", ya, "p3_ya", 4), ("w_pb", yb, "p3_yb", 4), ("w_pc", yc, "p3_yc", 8))):
                    w, wn_ = loadw(wn, m, kcn)
                    for kc in range(kcn):
                        P.mm(C.ps[b0 + bi][:], w[:, kc * 128:(kc + 1) * 128], src[:, kc, :], kc == 0, kc == kcn - 1,
                             reads=[wn_, srcn], writes=["ps%d" % (b0 + bi)])
                ts_ = []
                for bi in range(3):
                    t, tn = gtf()
                    P.tt("vector", t[:], C.ps[b0 + bi][:], sg[bi][:, m, :], ALU.mult, reads=["ps%d" % (b0 + bi), "p3_sg%d" % bi], writes=[tn])
                    ts_.append((t, tn))
                P.tt("gpsimd", ts_[0][0][:], ts_[0][0][:], ts_[1][0][:], ALU.add, reads=[ts_[0][1], ts_[1][1]], writes=[ts_[0][1]])
                P.tt("gpsimd", mg[:, m, :], ts_[0][0][:], ts_[2][0][:], ALU.add, reads=[ts_[0][1], ts_[2][1]], writes=["p3_mg"])
            for m in range(8):
                bk = 6 + (m % 2)
                w, wn_ = loadw("w_o", m, 8)
                for kc in range(8):
                    P.mm(C.ps[bk][:], w[:, kc * 128:(kc + 1) * 128], mg[:, kc, :], kc == 0, kc == 7, reads=[wn_, "p3_mg"], writes=["ps%d" % bk])
                P.tt("vector", x1[:, m, :], C.ps[bk][:], xc[:, m, :], ALU.add, reads=["ps%d" % bk, "p3_xc"], writes=["p3_x1"])
            P.act(sq8[:], x1[:], AF.Square, reads=["p3_x1"], writes=["p3_sq8"])
            for kc in range(8):
                P.mm(C.ps[6][:], ones, sq8[:, kc, :], kc == 0, kc == 7, reads=["cb", "p3_sq8"], writes=["ps6"])
            t, tn = gtf()
            P.act(t[:], C.ps[6][:], AF.Sqrt, reads=["ps6"], writes=[tn], bias=EPS, scale=1.0 / D)
            r_, rn = gtf()
            P.recip(r_[:], t[:], reads=[tn], writes=[rn])
            for kc in range(8):
                P.stt("vector", h2[:, kc, :], x1[:, kc, :], col(C, l, "ffn_g", kc), r_[:], ALU.mult, ALU.mult,
                      reads=["p3_x1", rn, "cols"], writes=["p3_h2"])
            for j in range(NJ):
                b0 = (j % 2) * 2
                for bi, wn in enumerate(("w_f1", "w_f3")):
                    w, wn_ = loadw(wn, j, 8)
                    for kc in range(8):
                        P.mm(C.ps[b0 + bi][:], w[:, kc * 128:(kc + 1) * 128], h2[:, kc, :], kc == 0, kc == 7,
                             reads=[wn_, "p3_h2"], writes=["ps%d" % (b0 + bi)])
                t, tn = gtf()
                P.act(t[:], C.ps[b0][:], AF.Silu, reads=["ps%d" % b0], writes=[tn])
                P.tt("vector", actb[:, j, :], t[:], C.ps[b0 + 1][:], ALU.mult, reads=[tn, "ps%d" % (b0 + 1)], writes=["p3_act"])
            for m in range(8):
                bk = 4 + (m % 2)
                i = nxt("w2", 2)
                P.dma("sync", w2sl[i][:], C.wbf["w_f2"][l, m], reads=[("wbf", "w_f2", l)], writes=["p3_w2%d" % i])
                for j in range(NJ):
                    P.mm(C.ps[bk][:], w2sl[i][:, j * 128:(j + 1) * 128], actb[:, j, :], j == 0, j == NJ - 1,
                         reads=["p3_w2%d" % i, "p3_act"], writes=["ps%d" % bk])
                P.tt("vector", xc[:, m, :], C.ps[bk][:], x1[:, m, :], ALU.add, reads=["ps%d" % bk, "p3_x1"], writes=["p3_xc"])
            P.dma("sync", S["xT"][:, :, sl].rearrange("k p t -> p k t"), xc[:], reads=["p3_xc"], writes=["xT"])
    P.barrier()


def phase_out(C, out_d):
    P, nc, T, S, NT = C.P, C.nc, C.T, C.S, C.NT
    ident = C.cf[:, CF_IDENT:CF_IDENT + 128]
    with ExitStack() as es:
        def sb(n, s, d):
            return C.sb(n, s, d, es)
        xc = [sb("po_x%d" % i, [128, 8, 512], F32) for i in range(2)]
        ot = [sb("po_o%d" % i, [128, 1024], F32) for i in range(2)]
        for c in range(NT):
            sl = slice(c * 512, (c + 1) * 512)
            x_, xn = xc[c % 2], "po_x%d" % (c % 2)
            P.dma("sync", x_[:], S["xT"][:, :, sl].rearrange("k p t -> p k t"), reads=["xT"], writes=[xn])
            for j in range(4):
                ti = c * 4 + j
                o_, on = ot[ti % 2], "po_o%d" % (ti % 2)
                for half in range(2):
                    bk = half + 2 * (ti % 2)
                    for q in range(4):
                        kc = half * 4 + q
                        P.op("tensor", lambda e, o=C.ps[bk][:, q * 128:(q + 1) * 128], i=x_[:, kc, j * 128:(j + 1) * 128]:
                             e.transpose(out=o, in_=i, identity=ident), reads=[xn, "cf"], writes=["ps%d" % bk])
                    P.copy("vector" if half == 0 else "scalar", o_[:, half * 512:(half + 1) * 512], C.ps[bk][:],
                           reads=["ps%d" % bk], writes=[on])
                P.dma("sync", out_d[ti * 128:(ti + 1) * 128, :], o_[:], reads=[on], writes=["out"])


def phase2_rglru(C, l):
    P, nc, T, S, NT = C.P, C.nc, C.T, C.S, C.NT
    with ExitStack() as es:
        def sb(n, s, d):
            return C.sb(n, s, d, es)
        wa = sb("rg_wa", [128, 1024], BF16)
        wx = sb("rg_wx", [128, 1024], BF16)
        P.dma("sync", wa[:], C.wbf["w_ca"][l, 0], reads=[("wbf", "w_ca", l)], writes=["rg_wa"])
        P.dma("sync", wx[:], C.wbf["w_cx"][l, 0], reads=[("wbf", "w_cx", l)], writes=["rg_wx"])
        cc = sb("rg_cc", [128, 8], F32)
        c2 = sb("rg_c2", [128, 8], F32)
        lo, _ = COL_LAY["lam"]
        lam = C.cols[:, l * NCOL + lo:l * NCOL + lo + 8]
        P.act(cc[:], lam, AF.Exp, reads=["cols"], writes=["rg_cc"], scale=-1.0)
        P.act(cc[:], cc[:], AF.Ln, reads=["rg_cc"], writes=["rg_cc"], bias=1.0)
        P.ts("vector", c2[:], cc[:], -16.0, None, ALU.mult, None, reads=["rg_cc"], writes=["rg_c2"])
        P.ts("vector", cc[:], cc[:], -8.0, None, ALU.mult, None, reads=["rg_cc", "rg_c2"], writes=["rg_cc"])
        xr = sb("rg_xr", [128, T], BF16)
        gcb = sb("rg_gc", [128, T], BF16)
        acc = sb("rg_acc", [128, T], F32)
        xcb = sb("rg_xcb", [128, T], BF16)
        A = sb("rg_A", [128, T], F32)
        Bt = sb("rg_B", [128, T], F32)
        I_ = sb("rg_I", [128, T], F32)
        H = sb("rg_H", [128, T], F32)
        yo = sb("rg_yo", [128, T], BF16)
        co, _ = COL_LAY["conv_w"]

        def cw(w, n):
            b = l * NCOL + co + w * 8 + n
            return C.cols[:, b:b + 1]

        for n in range(8):
            P.dma("sync", xr[:], S["cx"][n], reads=[("scr", "cx%d" % n)], writes=["rg_xr"])
            P.dma("sync", gcb[:], S["gc"][n], reads=[("scr", "cg%d" % n)], writes=["rg_gc"])
            P.ts("vector", acc[:], xr[:], cw(3, n), col(C, l, "conv_b", n), ALU.mult, ALU.add, reads=["rg_xr", "cols"], writes=["rg_acc"])
            for s_ in range(1, 4):
                P.stt("vector", acc[:, s_:], xr[:, 0:T - s_], cw(3 - s_, n), acc[:, s_:], ALU.mult, ALU.add,
                      reads=["rg_xr", "rg_acc", "cols"], writes=["rg_acc"])
            P.copy("gpsimd", xcb[:], acc[:], reads=["rg_acc"], writes=["rg_xcb"])
            import os
            RG_CUT = int(os.environ.get("RG_CUT", "9"))
            if RG_CUT <= 1:
                continue
            for c in range(NT):
                sl = slice(c * 512, (c + 1) * 512)
                b0 = (c % 2) * 2
                P.mm(C.ps[b0][:], wa[:, n * 128:(n + 1) * 128], xcb[:, sl], True, True, reads=["rg_wa", "rg_xcb"], writes=["ps%d" % b0])
                P.mm(C.ps[b0 + 1][:], wx[:, n * 128:(n + 1) * 128], xcb[:, sl], True, True, reads=["rg_wx", "rg_xcb"], writes=["ps%d" % (b0 + 1)])
                P.ts("vector", A[:, sl], C.ps[b0][:], col(C, l, "b_a", n), None, ALU.add, None, reads=["ps%d" % b0, "cols"], writes=["rg_A"])
                P.ts("vector", I_[:, sl], C.ps[b0 + 1][:], col(C, l, "b_x", n), None, ALU.add, None, reads=["ps%d" % (b0 + 1), "cols"], writes=["rg_I"])
                P.act(A[:, sl], A[:, sl], AF.Sigmoid, reads=["rg_A"], writes=["rg_A"])
                P.act(I_[:, sl], I_[:, sl], AF.Sigmoid, reads=["rg_I"], writes=["rg_I"])
            if RG_CUT <= 2:
                continue
            P.act(Bt[:], A[:], AF.Exp, reads=["rg_A", "rg_c2"], writes=["rg_B"], scale=c2[:, n:n + 1])
            P.act(A[:], A[:], AF.Exp, reads=["rg_A", "rg_cc"], writes=["rg_A"], scale=cc[:, n:n + 1])
            P.act(Bt[:], Bt[:], AF.Sqrt, reads=["rg_B"], writes=["rg_B"], bias=1.0, scale=-1.0)
            P.tt("gpsimd", I_[:], I_[:], acc[:], ALU.mult, reads=["rg_I", "rg_acc"], writes=["rg_I"])
            P.tt("vector", Bt[:], Bt[:], I_[:], ALU.mult, reads=["rg_B", "rg_I"], writes=["rg_B"])
            if RG_CUT <= 3:
                continue
            P.op("vector", lambda e: e.tensor_tensor_scan(out=H[:], data0=A[:], data1=Bt[:], initial=0.0, op0=ALU.mult, op1=ALU.add),
                 reads=["rg_A", "rg_B"], writes=["rg_H"])
            P.tt("gpsimd", yo[:], H[:], gcb[:], ALU.mult, reads=["rg_H", "rg_gc"], writes=["rg_yo"])
            P.dma("sync", S["yc"][n], yo[:], reads=["rg_yo"], writes=["yc"])
    P.barrier()


def phase2_mla(C, l):
    P, nc, T, S, NT = C.P, C.nc, C.T, C.S, C.NT
    cb = C.cb
    ones = cb[:, CB_ONES:CB_ONES + 128]
    rm96 = cb[:, CB_RM96:CB_RM96 + 128]
    scale = 96.0 ** -0.5
    NK = T // 128
    with ExitStack() as es:
        def sb(n, s, d):
            return C.sb(n, s, d, es)
        cqn = sb("ml_cqn", [128, 3, T], BF16)
        ckvn = sb("ml_ckvn", [128, 2, T], BF16)
        vall = sb("ml_v", [128, NK, 8, 65], BF16)
        kpe = sb("ml_kpe", [128, T], F32)
        cosM = sb("ml_cos", [128, T], F32)
        sinM = sb("ml_sin", [128, T], F32)
        Qh2 = [sb("ml_Q%d" % i, [128, T], BF16) for i in range(1)] * 2
        Kh2 = [sb("ml_K%d" % i, [128, T], BF16) for i in range(1)] * 2
        stg = [sb("ml_stg%d" % i, [128, 512], BF16) for i in range(4)]
        stgi = [0]
        wuq = sb("ml_wuq", [128, 3 * 768], BF16)
        wk = sb("ml_wk", [128, 2 * 512], BF16)
        wvv = sb("ml_wv", [128, 2 * 512], BF16)
        P.dma("sync", cqn[:], S["cqn"].rearrange("k p t -> p k t"), writes=["ml_cqn"])
        P.dma("sync", ckvn[:], S["ckvn"].rearrange("k p t -> p k t"), writes=["ml_ckvn"])
        P.dma("sync", kpe[64:96, :], S["kpe"], writes=["ml_kpe"])
        P.dma("sync", cosM[64:96, :], S["cosM"], writes=["ml_cos"])
        P.dma("sync", sinM[64:96, :], S["sinM"], writes=["ml_sin"])
        P.dma("sync", wuq[:], C.wbf["w_uq"][l, 0], reads=[("wbf", "w_uq", l)], writes=["ml_wuq"])
        P.dma("sync", wk[:], C.wbf["w_ukvk"][l, 0], reads=[("wbf", "w_ukvk", l)], writes=["ml_wk"])
        P.dma("sync", wvv[:], C.wbf["w_ukvv"][l, 0], reads=[("wbf", "w_ukvv", l)], writes=["ml_wv"])
        P.memset("vector", vall[:], 1.0, writes=["ml_v"])
        for i in range(NK):
            bk = 5 + (i % 2)
            for kc in range(2):
                P.mm(C.ps[bk][:], ckvn[:, kc, i * 128:(i + 1) * 128], wvv[:, kc * 512:(kc + 1) * 512], kc == 0, kc == 1,
                     reads=["ml_ckvn", "ml_wv"], writes=["ps%d" % bk])
            P.copy("scalar" if i % 2 == 0 else "vector", vall[:, i, :, 0:64], C.ps[bk][:].rearrange("p (h d) -> p h d", h=8),
                   reads=["ps%d" % bk], writes=["ml_v"])
        raw = [sb("ml_raw%d" % i, [128, 512], F32) for i in range(2)]
        tf = [sb("ml_tf%d" % i, [128, 512], F32) for i in range(6)]
        tb = [sb("ml_tb%d" % i, [128, 512], BF16) for i in range(6)]
        st = {"tf": 0, "tb": 0, "raw": 0}

        def gtf():
            i = st["tf"]; st["tf"] = (i + 1) % 6
            return tf[i], "ml_tf%d" % i

        def gtb():
            i = st["tb"]; st["tb"] = (i + 1) % 6
            return tb[i], "ml_tb%d" % i

        def norm_rope96(src, srcn, gname, dram_dst, sl):
            sq, sqn = gtb()
            P.act(sq[0:96, :], src, AF.Square, reads=[srcn], writes=[sqn])
            P.mm(C.ps[7][0:96, :], ones[0:96, 0:96], sq[0:96, :], True, True, reads=["cb", sqn], writes=["ps7"])
            t, tn = gtf()
            P.act(t[0:96, :], C.ps[7][0:96, :], AF.Sqrt, reads=["ps7"], writes=[tn], bias=EPS, scale=1.0 / 96)
            r_, rn = gtf()
            P.recip(r_[0:96, :], t[0:96, :], reads=[tn], writes=[rn])
            qn, qnn = gtf()
            P.stt("vector", qn[0:96, :], src, col(C, l, gname)[0:96, :], r_[0:96, :], ALU.mult, ALU.mult, reads=[srcn, rn, "cols"], writes=[qnn])
            qb, qbn = gtb()
            P.copy("scalar", qb[0:96, :], qn[0:96, :], reads=[qnn], writes=[qbn])
            si = stgi[0]; stgi[0] = (si + 1) % 4
            dst, dstn = stg[si], "ml_stg%d" % si
            P.copy("vector", dst[0:64, :], qb[0:64, :], reads=[qbn], writes=[dstn])
            P.mm(C.ps[7][0:96, :], rm96[0:96, 0:96], qb[0:96, :], True, True, reads=["cb", qbn], writes=["ps7"])
            t1, t1n = gtf()
            P.tt("vector", t1[64:96, :], qn[64:96, :], cosM[64:96, sl], ALU.mult, reads=[qnn, "ml_cos"], writes=[t1n])
            t2, t2n = gtf()
            P.tt("vector", t2[64:96, :], C.ps[7][64:96, :], sinM[64:96, sl], ALU.mult, reads=["ps7", "ml_sin"], writes=[t2n])
            P.tt("vector", dst[64:96, :], t1[64:96, :], t2[64:96, :], ALU.add, reads=[t1n, t2n], writes=[dstn])
            P.dma("sync", dram_dst[:, sl], dst[0:96, :], reads=[dstn], writes=["ml_dram"])

        import os
        ML_CUT = int(os.environ.get("ML_CUT", "9"))
        ones_f = C.cf[:, CF_ONES:CF_ONES + 64]
        cm = [cb[:, CB_CM + d * 512:CB_CM + (d + 1) * 512] for d in range(4)]
        rl = sb("ml_rl", [128, 512], F32)
        on_ = sb("ml_on", [64, 512], F32)
        yo = [sb("ml_yo%d" % i, [64, 512], BF16) for i in range(2)]
        for h in range(8 if ML_CUT > 1 else 0):
            for c in range(NT):
                sl = slice(c * 512, (c + 1) * 512)
                for kc in range(3):
                    P.mm(C.ps[5][0:96, :], wuq[:, kc * 768 + h * 96:kc * 768 + h * 96 + 96], cqn[:, kc, sl], kc == 0, kc == 2,
                         reads=["ml_wuq", "ml_cqn"], writes=["ps5"])
                norm_rope96(C.ps[5][0:96, :], "ps5", "bq_g", S["mq"][h], sl)
                for kc in range(2):
                    P.mm(C.ps[6][0:64, :], wk[:, kc * 512 + h * 64:kc * 512 + h * 64 + 64], ckvn[:, kc, sl], kc == 0, kc == 1,
                         reads=["ml_wk", "ml_ckvn"], writes=["ps6"])
                rw, rwn = raw[c % 2], "ml_raw%d" % (c % 2)
                P.copy("scalar", rw[0:64, :], C.ps[6][0:64, :], reads=["ps6"], writes=[rwn])
                P.copy("vector", rw[64:96, :], kpe[64:96, sl], reads=["ml_kpe"], writes=[rwn])
                norm_rope96(rw[0:96, :], rwn, "bk_g", S["mk"][h], sl)
        for h in range(8 if ML_CUT > 2 else 0):
            Qh, Kh = Qh2[h % 2], Kh2[h % 2]
            qn_, kn_ = "ml_Q0", "ml_K0"
            P.dma("sync", Qh[0:96, :], S["mq"][h], reads=["ml_dram"], writes=[qn_])
            P.dma("sync", Kh[0:96, :], S["mk"][h], reads=["ml_dram"], writes=[kn_])
            for c in range(NT):
                sl = slice(c * 512, (c + 1) * 512)
                nk = 4 * c + 4
                ob = 2 + (c % 2)
                for kc in range(nk):
                    sbk = kc % 2
                    if os.environ.get("MM_ALT"):
                        P.mm(C.ps[sbk][:], cqn[0:96, 0, kc * 128:(kc + 1) * 128], cqn[0:96, 1, sl], True, True, reads=["ml_cqn"], writes=["ps%d" % sbk])
                    else:
                        P.mm(C.ps[sbk][:], Kh[0:96, kc * 128:(kc + 1) * 128], Qh[0:96, sl], True, True, reads=[kn_, qn_], writes=["ps%d" % sbk])
                    pt, ptn = gtb()
                    P.act(pt[:], C.ps[sbk][:], AF.Exp, reads=["ps%d" % sbk], writes=[ptn], scale=scale)
                    LOOP_CUT = int(os.environ.get("LOOP_CUT", "9"))
                    if kc >= 4 * c and LOOP_CUT > 1:
                        P.tt(MASK_ENG, pt[:], pt[:], cm[kc - 4 * c], ALU.mult, reads=[ptn, "cb"], writes=[ptn])
                    if LOOP_CUT <= 2:
                        continue
                    P.mm(C.ps[ob][0:65, :], vall[:, kc, h, :], pt[:], kc == 0, kc == nk - 1, reads=["ml_v", ptn], writes=["ps%d" % ob])
                if ML_CUT <= 3:
                    continue
                P.recip(rl[64:65, :], C.ps[ob][64:65, :], reads=["ps%d" % ob], writes=["ml_rl"])
                P.mm(C.ps[4][0:64, :], ones_f[64:65, :], rl[64:65, :], True, True, reads=["cf", "ml_rl"], writes=["ps4"])
                P.copy("scalar", on_[:], C.ps[ob][0:64, :], reads=["ps%d" % ob], writes=["ml_on"])
                y_, yn = yo[c % 2], "ml_yo%d" % (c % 2)
                P.tt("vector", y_[:], on_[:], C.ps[4][0:64, :], ALU.mult, reads=["ml_on", "ps4"], writes=[yn])
                P.dma("sync", S["yb"][h, :, sl], y_[:], reads=[yn], writes=["yb"])
    P.barrier()


def phase2_nsa(C, l):
    P, nc, T, S, NT = C.P, C.nc, C.T, C.S, C.NT
    cb, cf = C.cb, C.cf
    ones = cb[:, CB_ONES:CB_ONES + 128]
    blk64 = cb[:, CB_BLK64:CB_BLK64 + 128]
    identb = cb[:, CB_ID:CB_ID + 128]
    ones_f = cf[:, CF_ONES:CF_ONES + 64]
    cm = [cb[:, CB_CM + d * 512:CB_CM + (d + 1) * 512] for d in range(4)]
    cmn = [cb[:, CB_CMN + d * 512:CB_CMN + (d + 1) * 512] for d in range(4)]
    n_cmp, n_sel, ncc = C.n_cmp, C.n_sel, C.ncc
    NK = T // 128
    scale = 0.125
    with ExitStack() as es:
        def sb(n, s, d):
            return C.sb(n, s, d, es)
        raw = [sb("na_raw%d" % z, [128, T], BF16) for z in range(2)]
        w1 = [sb("na_w1%d" % z, [128, 32 * 128], BF16) for z in range(2)]
        w2 = sb("na_w2", [128, 256], BF16)
        cpf = sb("na_cpf", [128, 64], F32)
        cpb = sb("na_cpb", [128, 64], BF16)
        bias = sb("na_bias", [128, 2], F32)
        hid = [sb("na_hid%d" % z, [128, 256], BF16) for z in range(2)]
        kcT = sb("na_kcT", [128, 256], BF16)
        vc = sb("na_vc", [128, ncc, 128], BF16)
        ovl = sb("na_ovl", [128, ncc * n_sel], BF16)
        P.dma("sync", raw[0][:], S["kcraw"], writes=["na_raw0"])
        P.dma("sync", raw[1][:], S["vcraw"], writes=["na_raw1"])
        for z in range(2):
            P.dma("sync", w1[z][:], C.wbf["w_c1"][l, z], reads=[("wbf", "w_c1", l)], writes=["na_w1%d" % z])
        P.dma("sync", w2[:], C.wbf["w_c2"][l, 0], reads=[("wbf", "w_c2", l)], writes=["na_w2"])
        P.dma("sync", cpf[:], C.cpos_in[l].rearrange("z p q -> p z q"), writes=["na_cpf"])
        P.dma("sync", ovl[:], C.ovl_in, writes=["na_ovl"])
        P.copy("vector", cpb[:], cpf[:], reads=["na_cpf"], writes=["na_cpb"])
        tf = [sb("na_tf%d" % i, [128, 512], F32) for i in range(6)]
        tb = [sb("na_tb%d" % i, [128, 512], BF16) for i in range(8)]
        st = {"tf": 0, "tb": 0}

        def gtf():
            i = st["tf"]; st["tf"] = (i + 1) % 6
            return tf[i], "na_tf%d" % i

        def gtb():
            i = st["tb"]; st["tb"] = (i + 1) % 8
            return tb[i], "na_tb%d" % i

        for z in range(2):
            for lp in range(32):
                P.mm(C.ps[7][:, 0:1], w1[z][:, lp * 128:(lp + 1) * 128], cpb[:, z * 32 + lp:z * 32 + lp + 1], lp == 0, lp == 31,
                     reads=["na_w1%d" % z, "na_cpb"], writes=["ps7"])
            P.copy("vector", bias[:, z:z + 1], C.ps[7][:, 0:1], reads=["ps7"], writes=["na_bias"])
            for lp in range(32):
                P.mm(C.ps[z][:, 0:n_cmp], w1[z][:, lp * 128:(lp + 1) * 128], raw[z][:, lp:lp + 16 * (n_cmp - 1) + 1:16], lp == 0, lp == 31,
                     reads=["na_w1%d" % z, "na_raw%d" % z], writes=["ps%d" % z])
            hp, hpn = gtf()
            P.ts("vector", hp[:, 0:n_cmp], C.ps[z][:, 0:n_cmp], bias[:, z:z + 1], None, ALU.add, None, reads=["ps%d" % z, "na_bias"], writes=[hpn])
            P.act(hid[z][:, 0:n_cmp], hp[:, 0:n_cmp], AF.Silu, reads=[hpn], writes=["na_hid%d" % z])
        P.mm(C.ps[2][:, 0:n_cmp], w2[:, 0:128], hid[0][:, 0:n_cmp], True, True, reads=["na_w2", "na_hid0"], writes=["ps2"])
        sq, sqn = gtb()
        P.act(sq[:, 0:n_cmp], C.ps[2][:, 0:n_cmp], AF.Square, reads=["ps2"], writes=[sqn])
        P.mm(C.ps[3][:, 0:n_cmp], blk64, sq[:, 0:n_cmp], True, True, reads=["cb", sqn], writes=["ps3"])
        t, tn = gtf()
        P.act(t[:, 0:n_cmp], C.ps[3][:, 0:n_cmp], AF.Sqrt, reads=["ps3"], writes=[tn], bias=EPS, scale=1.0 / 64)
        r_, rn = gtf()
        P.recip(r_[:, 0:n_cmp], t[:, 0:n_cmp], reads=[tn], writes=[rn])
        P.stt("vector", kcT[:, 0:n_cmp], C.ps[2][:, 0:n_cmp], col(C, l, "ak_g"), r_[:, 0:n_cmp], ALU.mult, ALU.mult,
              reads=["ps2", rn, "cols"], writes=["na_kcT"])
        for cc_ in range(ncc):
            nn = min(128, n_cmp - cc_ * 128)
            P.mm(C.ps[4][0:nn, 0:128], hid[1][:, cc_ * 128:cc_ * 128 + nn], w2[:, 128:256], True, True, reads=["na_hid1", "na_w2"], writes=["ps4"])
            P.copy("vector", vc[0:nn, cc_, :], C.ps[4][0:nn, 0:128], reads=["ps4"], writes=["na_vc"])
        qt = [sb("na_q%d" % i, [128, T], BF16) for i in range(2)]
        mk = [sb("na_mk%d" % i, [128, ncc, 512], BF16) for i in range(2)]
        impacc = sb("na_imp", [128, NK, n_sel], F32)
        rl = sb("na_rl", [128, 512], F32)
        ocb = [sb("na_oc%d" % i, [64, 512], BF16) for i in range(2)]
        adj = sb("na_adj", [128, n_sel], F32)
        adj2 = sb("na_adj2", [128, n_sel], F32)
        m8 = sb("na_m8", [128, 16], F32)
        selb = sb("na_sel", [128, 4, 64], BF16)
        nmo = [sb("na_nmo%d" % i, [64, 512], BF16) for i in range(2)]
        for g in range(2):
            r0 = 64 * g
            for hh in range(4):
                h = g * 4 + hh
                q_, qn_ = qt[h % 2], "na_q%d" % (h % 2)
                P.dma("sync", q_[r0:r0 + 64, :], S["qc"][h], writes=[qn_])
                for c in range(NT):
                    sl = slice(c * 512, (c + 1) * 512)
                    tmax = c * 512 + 511
                    vis = [cc_ for cc_ in range(ncc) if 16 * (cc_ * 128) + 31 <= tmax]
                    m_, mn_ = mk[c % 2], "na_mk%d" % (c % 2)
                    if hh == 0 or True:
                        P.dma("sync", m_[:], C.maskc_in[:, :, sl].rearrange("c p t -> p c t"), writes=[mn_])
                    ems = []
                    for cc_ in vis:
                        nn = min(128, n_cmp - cc_ * 128)
                        sbk = cc_ % 2
                        P.mm(C.ps[sbk][0:nn, :], kcT[r0:r0 + 64, cc_ * 128:cc_ * 128 + nn], q_[r0:r0 + 64, sl], True, True,
                             reads=["na_kcT", qn_], writes=["ps%d" % sbk])
                        e_, en_ = gtb()
                        P.act(e_[0:nn, :], C.ps[sbk][0:nn, :], AF.Exp, reads=["ps%d" % sbk], writes=[en_], scale=scale)
                        P.tt("vector", e_[0:nn, :], e_[0:nn, :], m_[0:nn, cc_, :], ALU.mult, reads=[en_, mn_], writes=[en_])
                        ems.append((cc_, nn, e_, en_))
                    for i, (cc_, nn, e_, en_) in enumerate(ems):
                        P.mm(C.ps[2][:], ones[0:nn, :], e_[0:nn, :], i == 0, i == len(ems) - 1, reads=["cb", en_], writes=["ps2"])
                    P.ts("vector", rl[:], C.ps[2][:], 1e-30, None, ALU.add, None, reads=["ps2"], writes=["na_rl"])
                    P.recip(rl[:], rl[:], reads=["na_rl"], writes=["na_rl"])
                    for (cc_, nn, e_, en_) in ems:
                        P.tt("vector", e_[0:nn, :], e_[0:nn, :], rl[0:nn, :], ALU.mult, reads=[en_, "na_rl"], writes=[en_])
                    for i, (cc_, nn, e_, en_) in enumerate(ems):
                        P.mm(C.ps[3][0:64, :], vc[0:nn, cc_, r0:r0 + 64], e_[0:nn, :], i == 0, i == len(ems) - 1, reads=["na_vc", en_], writes=["ps3"])
                    o_, on_ = ocb[c % 2], "na_oc%d" % (c % 2)
                    P.copy("scalar", o_[:], C.ps[3][0:64, :], reads=["ps3"], writes=[on_])
                    P.dma("sync", S["oc"][h, :, sl], o_[:], reads=[on_], writes=["oc"])
                    for j in range(4):
                        for i, (cc_, nn, e_, en_) in enumerate(ems):
                            P.mm(C.ps[4][:, j * n_sel:(j + 1) * n_sel], e_[0:nn, j * 128:(j + 1) * 128], ovl[0:nn, cc_ * n_sel:(cc_ + 1) * n_sel],
                                 i == 0, i == len(ems) - 1, reads=[en_, "na_ovl"], writes=["ps4"])
                    dst = impacc[:, c * 4:(c + 1) * 4, :]
                    src = C.ps[4][:, 0:4 * n_sel].rearrange("p (j s) -> p j s", j=4)
                    if hh == 0:
                        P.copy("vector", dst, src, reads=["ps4"], writes=["na_imp"])
                    else:
                        P.tt("vector", dst, dst, src, ALU.add, reads=["ps4", "na_imp"], writes=["na_imp"])
            for c in range(NT):
                sl = slice(c * 512, (c + 1) * 512)
                for j in range(4):
                    i = c * 4 + j
                    if n_sel > 16:
                        w0 = CF_TOPW + 64 - 2 * i
                        P.tt("vector", adj[:], impacc[:, i, :], cf[:, w0:w0 + n_sel], ALU.add, reads=["na_imp", "cf"], writes=["na_adj"])
                        P.memset("vector", adj[:, 0:1], BIG, writes=["na_adj"])
                        P.op("vector", lambda e: e.max(out=m8[:, 0:8], in_=adj[:]), reads=["na_adj"], writes=["na_m8"])
                        P.ts("vector", adj2[:], adj[:], m8[:, 7:8], None, ALU.is_ge, None, reads=["na_adj", "na_m8"], writes=["na_adj2"])
                        P.stt("vector", adj2[:], adj2[:], -BIG, adj[:], ALU.mult, ALU.add, reads=["na_adj2", "na_adj"], writes=["na_adj2"])
                        P.op("vector", lambda e: e.max(out=m8[:, 8:16], in_=adj2[:]), reads=["na_adj2"], writes=["na_m8"])
                        P.ts("vector", adj2[:], adj[:], m8[:, 15:16], None, ALU.is_ge, None, reads=["na_adj", "na_m8"], writes=["na_adj2"])
                        P.ts("vector", selb[:, j, 0:n_sel], adj2[:], -1.0, 30000.0, ALU.add, ALU.mult, reads=["na_adj2"], writes=["na_sel"])
                    else:
                        P.memset("vector", selb[:, j, :], 0.0, writes=["na_sel"])
                    P.mm(C.ps[5][0:n_sel, j * 128:(j + 1) * 128], selb[:, j, 0:n_sel], identb, True, True, reads=["na_sel", "cb"], writes=["ps5"])
                o_, on_ = nmo[c % 2], "na_nmo%d" % (c % 2)
                if n_sel < 64:
                    P.memset("gpsimd", o_[:], 0.0, writes=[on_])
                P.copy("scalar", o_[0:n_sel, :], C.ps[5][0:n_sel, :], reads=["ps5"], writes=[on_])
                P.dma("sync", S["nmask"][g, :, sl], o_[:], reads=[on_], writes=["nmask"])
    P.barrier()
    with ExitStack() as es:
        def sb(n, s, d):
            return C.sb(n, s, d, es)
        Qa = [sb("nb_Q%d" % i, [128, T], BF16) for i in range(2)]
        Ka = sb("nb_K", [128, T], BF16)
        Kw = sb("nb_Kw", [64, T], BF16)
        Vs = sb("nb_Vs", [128, NK, 65], BF16)
        Vw = sb("nb_Vw", [128, NK, 65], BF16)
        tb = [sb("nb_tb%d" % i, [128, 512], BF16) for i in range(6)]
        grow = sb("nb_g", [128, 3, 512], F32)
        wrow = sb("nb_w", [128, 2, 512], F32)
        ocl = sb("nb_oc", [64, 512], BF16)
        os_ = sb("nb_os", [64, 512], F32)
        ow_ = sb("nb_ow", [64, 512], F32)
        y1 = sb("nb_y1", [64, 512], F32)
        y2 = sb("nb_y2", [64, 512], F32)
        yo = [sb("nb_yo%d" % i, [64, 512], BF16) for i in range(2)]
        st = {"tb": 0}

        def gtb():
            i = st["tb"]; st["tb"] = (i + 1) % 6
            return tb[i], "nb_tb%d" % i

        P.memset("vector", Vs[:], 1.0, writes=["nb_Vs"])
        P.memset("vector", Vw[:], 1.0, writes=["nb_Vw"])
        P.dma("sync", Ka[64:128, :], C.emat_in, writes=["nb_K"])
        for g in range(2):
            P.dma("sync", Ka[0:64, :], S["ks"][g], writes=["nb_K"])
            P.dma("sync", Kw[:], S["kw"][g], writes=["nb_Kw"])
            P.dma("sync", Vs[:, :, 0:64], S["vsw"][:, g * 64:(g + 1) * 64].rearrange("(k p) d -> p k d", p=128), writes=["nb_Vs"])
            P.dma("sync", Vw[:, :, 0:64], S["vsw"][:, 128 + g * 64:128 + (g + 1) * 64].rearrange("(k p) d -> p k d", p=128), writes=["nb_Vw"])
            for hh in range(4):
                h = g * 4 + hh
                Q_, Qn_ = Qa[h % 2], "nb_Q%d" % (h % 2)
                P.dma("sync", Q_[0:64, :], S["qr"][h], writes=[Qn_])
                P.dma("sync", Q_[64:128, :], S["nmask"][g], writes=[Qn_])
                for c in range(NT):
                    sl = slice(c * 512, (c + 1) * 512)
                    nk = 4 * c + 4
                    for kc in range(nk):
                        sbk = kc % 2
                        P.mm(C.ps[sbk][:], Ka[:, kc * 128:(kc + 1) * 128], Q_[:, sl], True, True, reads=["nb_K", Qn_], writes=["ps%d" % sbk])
                        pt, ptn = gtb()
                        P.act(pt[:], C.ps[sbk][:], AF.Exp, reads=["ps%d" % sbk], writes=[ptn], scale=scale)
                        if kc >= 4 * c:
                            P.tt("vector", pt[:], pt[:], cm[kc - 4 * c], ALU.mult, reads=[ptn, "cb"], writes=[ptn])
                        P.mm(C.ps[2][0:65, :], Vs[:, kc, :], pt[:], kc == 0, kc == nk - 1, reads=["nb_Vs", ptn], writes=["ps2"])
                    k0 = max(0, 4 * c - 4)
                    for kc in range(k0, nk):
                        sbk = kc % 2
                        P.mm(C.ps[sbk][:], Kw[0:64, kc * 128:(kc + 1) * 128], Q_[0:64, sl], True, True, reads=["nb_Kw", Qn_], writes=["ps%d" % sbk])
                        pt, ptn = gtb()
                        P.act(pt[:], C.ps[sbk][:], AF.Exp, reads=["ps%d" % sbk], writes=[ptn], scale=scale)
                        d = kc - 4 * c
                        mk_ = cm[d] if d >= 0 else cmn[d + 4]
                        P.tt("vector", pt[:], pt[:], mk_, ALU.mult, reads=[ptn, "cb"], writes=[ptn])
                        P.mm(C.ps[3][0:65, :], Vw[:, kc, :], pt[:], kc == k0, kc == nk - 1, reads=["nb_Vw", ptn], writes=["ps3"])
                    P.dma("sync", grow[64:65, :, :], S["gT"][h * 3:h * 3 + 3, sl].rearrange("(o r) t -> o r t", o=1), writes=["nb_g"])
                    P.dma("sync", ocl[:], S["oc"][h, :, sl], writes=["nb_oc"])
                    P.recip(wrow[64:65, 0, :], C.ps[2][64:65, :], reads=["ps2"], writes=["nb_w"])
                    P.recip(wrow[64:65, 1, :], C.ps[3][64:65, :], reads=["ps3"], writes=["nb_w"])
                    P.tt("vector", wrow[64:65, :, :], wrow[64:65, :, :], grow[64:65, 1:3, :], ALU.mult, reads=["nb_w", "nb_g"], writes=["nb_w"])
                    P.mm(C.ps[4][0:64, :], ones_f[64:65, :], grow[64:65, 0, :], True, True, reads=["cf", "nb_g"], writes=["ps4"])
                    P.mm(C.ps[5][0:64, :], ones_f[64:65, :], wrow[64:65, 0, :], True, True, reads=["cf", "nb_w"], writes=["ps5"])
                    P.mm(C.ps[6][0:64, :], ones_f[64:65, :], wrow[64:65, 1, :], True, True, reads=["cf", "nb_w"], writes=["ps6"])
                    P.copy("scalar", os_[:], C.ps[2][0:64, :], reads=["ps2"], writes=["nb_os"])
                    P.copy("scalar", ow_[:], C.ps[3][0:64, :], reads=["ps3"], writes=["nb_ow"])
                    P.tt("vector", y1[:], ocl[:], C.ps[4][0:64, :], ALU.mult, reads=["nb_oc", "ps4"], writes=["nb_y1"])
                    P.tt("vector", y2[:], os_[:], C.ps[5][0:64, :], ALU.mult, reads=["nb_os", "ps5"], writes=["nb_y2"])
                    P.tt("gpsimd", y1[:], y1[:], y2[:], ALU.add, reads=["nb_y1", "nb_y2"], writes=["nb_y1"])
                    P.tt("vector", y2[:], ow_[:], C.ps[6][0:64, :], ALU.mult, reads=["nb_ow", "ps6", "nb_y1"], writes=["nb_y2"])
                    y_, yn = yo[c % 2], "nb_yo%d" % (c % 2)
                    P.tt("gpsimd", y_[:], y1[:], y2[:], ALU.add, reads=["nb_y1", "nb_y2"], writes=[yn])
                    P.dma("sync", S["ya"][h, :, sl], y_[:], reads=[yn], writes=["ya"])
    P.barrier()


def prepare_weights(inp, nl=NL):
    f = lambda a: np.asarray(a, dtype=np.float32)
    out = {}
    w_in = f(inp["w_in"])
    img = np.zeros((nl, NG1, 128, 1024), np.float32)
    wv = np.zeros((nl, 1, 128, 8 * 256), np.float32)
    for l in range(nl):
        for gi, (nm, pieces) in enumerate(IN_GROUPS):
            Wg = np.zeros((1024, 128), np.float32)
            for cs, n, dst in pieces:
                Wg[:, dst:dst + n] = w_in[l][:, cs:cs + n]
            img[l, gi] = _img_km(Wg, 8)
        Wv = np.concatenate([w_in[l][:, 896:1024], w_in[l][:, 1152:1280]], axis=1)
        wv[l, 0] = _img_km(Wv, 8)
    out["w_in"], out["w_v"] = img, wv
    out["w_uq"] = np.stack([_img_km(f(inp["b_w_uq"])[l], 3)[None] for l in range(nl)])
    ukv = f(inp["b_w_ukv"]).reshape(-1, 256, 8, 128)
    out["w_ukvk"] = np.stack([_img_km(np.ascontiguousarray(ukv[l][:, :, :64]).reshape(256, 512), 2)[None] for l in range(nl)])
    out["w_ukvv"] = np.stack([_img_km(np.ascontiguousarray(ukv[l][:, :, 64:]).reshape(256, 512), 2)[None] for l in range(nl)])
    out["w_ca"] = np.stack([f(inp["c_w_a"])[l].transpose(1, 0, 2).reshape(1, 128, 1024) for l in range(nl)])
    out["w_cx"] = np.stack([f(inp["c_w_x"])[l].transpose(1, 0, 2).reshape(1, 128, 1024) for l in range(nl)])

    def per_m(w, kc):
        return np.stack([_img_km(np.ascontiguousarray(w[:, m * 128:(m + 1) * 128]), kc) for m in range(w.shape[1] // 128)])
    out["w_pa"] = np.stack([per_m(f(inp["w_pa"])[l], 4) for l in range(nl)])
    out["w_pb"] = np.stack([per_m(f(inp["w_pb"])[l], 4) for l in range(nl)])
    out["w_pc"] = np.stack([per_m(f(inp["w_pc"])[l], 8) for l in range(nl)])
    out["w_o"] = np.stack([per_m(f(inp["w_o"])[l], 8) for l in range(nl)])
    out["w_f1"] = np.stack([per_m(f(inp["ffn_w1"])[l], 8) for l in range(nl)])
    out["w_f3"] = np.stack([per_m(f(inp["ffn_w3"])[l], 8) for l in range(nl)])
    out["w_f2"] = np.stack([per_m(f(inp["ffn_w2"])[l], NJ) for l in range(nl)])
    c1 = f(inp["a_cmp_w1"]).reshape(-1, 2, 32, 64, 64)
    w_c1 = np.zeros((nl, 2, 128, 32, 128), np.float32)
    for g in range(2):
        w_c1[:, :, g * 64:(g + 1) * 64, :, g * 64:(g + 1) * 64] = c1[:nl].transpose(0, 1, 3, 2, 4)
    out["w_c1"] = w_c1.reshape(nl, 2, 128, 32 * 128)
    c2 = f(inp["a_cmp_w2"])
    w_c2 = np.zeros((nl, 1, 128, 2, 128), np.float32)
    for g in range(2):
        w_c2[:, 0, g * 64:(g + 1) * 64, :, g * 64:(g + 1) * 64] = c2[:nl].transpose(0, 2, 1, 3)
    out["w_c2"] = w_c2.reshape(nl, 1, 128, 256)
    cp = f(inp["a_cmp_pos"])[:nl]
    cpi = np.zeros((nl, 2, 128, 32), np.float32)
    for g in range(2):
        cpi[:, :, g * 64:(g + 1) * 64, :] = cp.transpose(0, 1, 3, 2)
    out["cmp_pos"] = cpi
    cols = np.zeros((128, nl * NCOL), np.float32)
    for l in range(nl):
        def put(name, arr):
            o, n = COL_LAY[name]
            cols[:, l * NCOL + o:l * NCOL + o + n] = arr
        put("mix_g", _cols(f(inp["mix_norm_g"])[l]))
        put("ffn_g", _cols(f(inp["ffn_norm_g"])[l]))
        put("aq_g", np.tile(f(inp["a_q_norm_g"])[l], 2)[:, None])
        put("ak_g", np.tile(f(inp["a_k_norm_g"])[l], 2)[:, None])
        put("bcq_g", _cols(f(inp["b_cq_norm_g"])[l]))
        put("bckv_g", _cols(f(inp["b_ckv_norm_g"])[l]))
        pad = np.zeros(128, np.float32); pad[:96] = f(inp["b_q_norm_g"])[l]
        put("bq_g", pad[:, None])
        pad = np.zeros(128, np.float32); pad[:96] = f(inp["b_k_norm_g"])[l]
        put("bk_g", pad[:, None])
        cw = f(inp["c_conv_w"])[l]
        put("conv_w", np.concatenate([_cols(cw[w]) for w in range(4)], axis=1))
        put("conv_b", _cols(f(inp["c_conv_b"])[l]))
        put("b_a", _cols(f(inp["c_b_a"])[l]))
        put("b_x", _cols(f(inp["c_b_x"])[l]))
        put("lam", _cols(f(inp["c_lambda"])[l]))
    out["cols"] = cols
    return out


_NC_CACHE = {}


def kernel(**inputs):
    x = np.asarray(inputs["x"], dtype=np.float32)
    pos = np.asarray(inputs["positions"], dtype=np.int32)
    B, T, _ = x.shape
    shared = prepare_weights(inputs, NL)
    shared.update(make_constants(T))
    key = (T, NL)
    if key not in _NC_CACHE:
        _NC_CACHE[key] = build_program(T, NL)
    nc = _NC_CACHE[key]
    in_maps = []
    for b in range(B):
        m = dict(shared)
        m["x"] = np.ascontiguousarray(x[b])
        m["pos"] = np.ascontiguousarray(pos[b:b + 1])
        in_maps.append(m)
    res = run_bass_kernel_spmd(nc, in_maps, core_ids=list(range(B)))
    return np.stack([np.asarray(r["out"], dtype=np.float32) for r in res.results], axis=0)
```
